# Optimizing a Trainium2 kernel written in Bass

```python
import jax, jax.numpy as jnp
from jax import lax
import numpy as np

D_MODEL = 1024
BATCH = 8
SEQ = 2048
DEPTH = 2

PLE_DIM = 256
D_FF = ((8 * D_MODEL // 3 + 255) // 256) * 256
D_MIX = D_MODEL
W_A = D_MIX // 2
GM_GROUPS = 4
GM_DH = W_A // GM_GROUPS
CHUNK = 128
W_B = D_MIX - W_A
SC_HEADS = 8
SC_WIDTH = 3
W_C = D_MIX // 2
POOL_WINDOWS = (2, 4, 8, 16)
POOL_GW = W_C // len(POOL_WINDOWS)
W_D = D_MIX - W_C
CV_HEADS = 8
CV_WIDTH = 31
N_EVEN = (DEPTH + 1) // 2
N_ODD = DEPTH // 2
IN_EVEN = 2 * W_A + 3 * W_B
IN_ODD = W_C + 2 * W_D
RMS_EPS = 1e-6
LN_EPS = 1e-5

kernel_name = "hybrid_gmlp_shortconv_pool_conformer_macaron"


def rmsnorm(x, g):
    xf = x.astype(jnp.float32)
    y = xf * lax.rsqrt(jnp.mean(xf * xf, axis=-1, keepdims=True) + RMS_EPS)
    return (y * g.astype(jnp.float32)).astype(x.dtype)


def layernorm(x, g, b):
    xf = x.astype(jnp.float32)
    mu = jnp.mean(xf, axis=-1, keepdims=True)
    xc = xf - mu
    var = jnp.mean(xc * xc, axis=-1, keepdims=True)
    y = xc * lax.rsqrt(var + LN_EPS) * g.astype(jnp.float32) + b.astype(jnp.float32)
    return y.astype(x.dtype)


def swiglu(x, w_gu, w_down):
    g, u = jnp.split(x @ w_gu, 2, axis=-1)
    return (jax.nn.silu(g) * u) @ w_down


def causal_dwconv(x, w):
    k, c = w.shape
    return lax.conv_general_dilated(
        x, w[:, None, :].astype(x.dtype), window_strides=(1,), padding=[(k - 1, 0)],
        dimension_numbers=("NWC", "WIO", "NWC"), feature_group_count=c)


def gmlp_spatial_gating(uv, ln_g, ln_b, w_s, b_s):
    u, v = jnp.split(jax.nn.gelu(uv), 2, axis=-1)
    bsz, s, _ = v.shape
    vg = layernorm(v.reshape(bsz, s, GM_GROUPS, GM_DH),
                   ln_g.reshape(GM_GROUPS, GM_DH), ln_b.reshape(GM_GROUPS, GM_DH))
    vc = vg.reshape(bsz, s // CHUNK, CHUNK, GM_GROUPS, GM_DH)
    w = jnp.tril(w_s)
    mixed = jnp.einsum("gts,bcsgd->bctgd", w, vc) + b_s.T[:, :, None]
    return u * mixed.reshape(bsz, s, W_A)


def short_conv_mixer(bg, cg, xv, w):
    return bg * causal_dwconv(cg * xv, w)


def multiscale_pool(x):
    s = x.shape[1]
    xf = x.astype(jnp.float32)
    cs = jnp.cumsum(xf, axis=1)
    cnt_base = jnp.arange(1, s + 1, dtype=jnp.float32)[None, :, None]
    outs = []
    for gi, win in enumerate(POOL_WINDOWS):
        sl = slice(gi * POOL_GW, (gi + 1) * POOL_GW)
        c = cs[:, :, sl]
        prev = jnp.pad(c[:, : s - win], ((0, 0), (win, 0), (0, 0)))
        count = jnp.minimum(cnt_base, float(win))
        outs.append((c - prev) / count - xf[:, :, sl])
    return jnp.concatenate(outs, axis=-1).astype(x.dtype)


def pool_mixer(zc, pool_w, pool_scale):
    bsz, s, _ = zc.shape
    pooled = multiscale_pool(zc).reshape(bsz, s, len(POOL_WINDOWS), POOL_GW)
    y = jnp.einsum("bsgc,gcd->bsgd", pooled, pool_w).reshape(bsz, s, W_C)
    return y * pool_scale


def conformer_conv(zd, cv_w, cv_b, ln_g, ln_b):
    a, g = jnp.split(zd, 2, axis=-1)
    h = causal_dwconv(a * jax.nn.sigmoid(g), cv_w) + cv_b
    return jax.nn.silu(layernorm(h, ln_g, ln_b))


def setup_inputs(seed: int = 0) -> dict:
    key = jax.random.key(seed)
    ks = iter(jax.random.split(key, 40))
    f32 = jnp.float32

    def nrm(shape, scale):
        return jax.random.normal(next(ks), shape, f32) * scale

    def gain(shape):
        return 1.0 + nrm(shape, 0.02)

    return {
        "x": nrm((BATCH, SEQ, D_MODEL), 1.0),
        "p": nrm((DEPTH, BATCH, SEQ, PLE_DIM), 1.0),
        "ffn1_norm": gain((DEPTH, D_MODEL)),
        "ffn1_w_gu": nrm((DEPTH, D_MODEL, 2 * D_FF), D_MODEL ** -0.5),
        "ffn1_w_down": nrm((DEPTH, D_FF, D_MODEL), D_FF ** -0.5),
        "mix_norm": gain((DEPTH, D_MODEL)),
        "ffn2_norm": gain((DEPTH, D_MODEL)),
        "ffn2_w_gu": nrm((DEPTH, D_MODEL, 2 * D_FF), D_MODEL ** -0.5),
        "ffn2_w_down": nrm((DEPTH, D_FF, D_MODEL), D_FF ** -0.5),
        "ple_norm": gain((DEPTH, D_MODEL)),
        "ple_w_gate": nrm((DEPTH, D_MODEL, D_MODEL), D_MODEL ** -0.5),
        "ple_w_up": nrm((DEPTH, PLE_DIM, D_MODEL), PLE_DIM ** -0.5),
        "ab_w_in": nrm((N_EVEN, D_MODEL, IN_EVEN), D_MODEL ** -0.5),
        "gm_ln_g": gain((N_EVEN, W_A)),
        "gm_ln_b": nrm((N_EVEN, W_A), 0.02),
        "gm_w_s": nrm((N_EVEN, GM_GROUPS, CHUNK, CHUNK), CHUNK ** -0.5),
        "gm_b_s": gain((N_EVEN, GM_GROUPS, CHUNK)),
        "sc_w": nrm((N_EVEN, SC_WIDTH, W_B), SC_WIDTH ** -0.5),
        "ab_w_out": nrm((N_EVEN, W_A + W_B, D_MODEL), (W_A + W_B) ** -0.5),
        "cd_w_in": nrm((N_ODD, D_MODEL, IN_ODD), D_MODEL ** -0.5),
        "pool_w": nrm((N_ODD, len(POOL_WINDOWS), POOL_GW, POOL_GW), POOL_GW ** -0.5),
        "pool_scale": gain((N_ODD, W_C)),
        "cv_w": nrm((N_ODD, CV_WIDTH, W_D), CV_WIDTH ** -0.5),
        "cv_b": nrm((N_ODD, W_D), 0.02),
        "cv_ln_g": gain((N_ODD, W_D)),
        "cv_ln_b": nrm((N_ODD, W_D), 0.02),
        "cd_w_out": nrm((N_ODD, W_C + W_D, D_MODEL), (W_C + W_D) ** -0.5),
        "final_norm": gain((D_MODEL,)),
    }


def reference(x, p, ffn1_norm, ffn1_w_gu, ffn1_w_down, mix_norm, ffn2_norm, ffn2_w_gu,
              ffn2_w_down, ple_norm, ple_w_gate, ple_w_up, ab_w_in, gm_ln_g, gm_ln_b,
              gm_w_s, gm_b_s, sc_w, ab_w_out, cd_w_in, pool_w, pool_scale, cv_w, cv_b,
              cv_ln_g, cv_ln_b, cd_w_out, final_norm):
    h = x
    for i in range(DEPTH):
        h = h + 0.5 * swiglu(rmsnorm(h, ffn1_norm[i]), ffn1_w_gu[i], ffn1_w_down[i])
        hn = rmsnorm(h, mix_norm[i])
        j = i // 2
        if i % 2 == 0:
            z = hn @ ab_w_in[j]
            uv = z[..., : 2 * W_A]
            bg, cg, xv = jnp.split(z[..., 2 * W_A:], 3, axis=-1)
            ya = gmlp_spatial_gating(uv, gm_ln_g[j], gm_ln_b[j], gm_w_s[j], gm_b_s[j])
            yb = short_conv_mixer(bg, cg, xv, sc_w[j])
            y = jnp.concatenate([ya, yb], axis=-1) @ ab_w_out[j]
        else:
            z = hn @ cd_w_in[j]
            yc = pool_mixer(z[..., :W_C], pool_w[j], pool_scale[j])
            yd = conformer_conv(z[..., W_C:], cv_w[j], cv_b[j], cv_ln_g[j], cv_ln_b[j])
            y = jnp.concatenate([yc, yd], axis=-1) @ cd_w_out[j]
        h = h + y
        h = h + 0.5 * swiglu(rmsnorm(h, ffn2_norm[i]), ffn2_w_gu[i], ffn2_w_down[i])
        gate = jax.nn.sigmoid(rmsnorm(h, ple_norm[i]) @ ple_w_gate[i])
        h = h + gate * (p[i] @ ple_w_up[i])
    return rmsnorm(h, final_norm)
```

```python
import numpy as np
import concourse.bass as bass
import concourse.mybir as mybir
from concourse.bass_utils import run_bass_kernel_spmd

F32 = mybir.dt.float32
BF16 = mybir.dt.bfloat16
AF = mybir.ActivationFunctionType
ALU = mybir.AluOpType

D = 1024
S = 2048
NB = 8
KC = 8
DFF = 2816
FC = 22
NT = 4
TW = 512
PLE = 256
RMS_EPS = 1e-6
LN_EPS = 1e-5
CVW = 31
GPAD = 32
XPAD = 16

SB_BASE = 16512
SB_END = 229344

CST = {}
_off = 0
for _n, _w in [("gains", 72), ("lng", 4), ("lnb", 4), ("scw", 12), ("pscale", 4), ("cvw", 124),
               ("cvb", 4), ("cvg", 4), ("cvbt", 4), ("invc", 64), ("ident", 128), ("mask", 128),
               ("bsb", 512)]:
    CST[_n] = (_off, _w)
    _off += _w
CST_N = _off


class Op:
    __slots__ = ("eng", "fn", "deps", "rawdeps", "is_dma", "sem", "val", "idx", "signal", "cnt")

    def __init__(self, eng, fn, is_dma=False):
        self.eng = eng
        self.fn = fn
        self.deps = set()
        self.rawdeps = set()
        self.is_dma = is_dma
        self.sem = None
        self.val = None
        self.idx = None
        self.signal = False
        self.cnt = None


class Buf:
    _uid = 0

    def __init__(self, name, off, size, t, inherit):
        Buf._uid += 1
        self.uid = Buf._uid
        self.name = name
        self.off = off
        self.size = size
        self.t = t
        self.inherit = inherit
        self.keys = set()

    def k(self, *sub):
        key = (self,) + sub
        self.keys.add(key)
        return key


class Prog:
    ENGS = ("pe", "act", "dve", "pool", "sp")

    def __init__(self, nc):
        self.nc = nc
        self.ops = {e: [] for e in self.ENGS}
        self.res = {}
        self.esem = {e: nc.alloc_semaphore("sem_" + e) for e in ("pe", "act", "dve", "pool")}
        self.dq = {}
        for q, n in (("pool", 8), ("sp", 8)):
            self.dq[q] = dict(sems=[nc.alloc_semaphore("dma_%s%d" % (q, i)) for i in range(n)],
                              cnt=[0] * n, last=[None] * n, nxt=0)
        self.banks = [nc.alloc_psum_tensor("psb%d" % i, [128, 512], F32) for i in range(8)]
        self.bank_i = 0
        self.free = [(SB_BASE, SB_END)]
        self.pending = []
        self.peak = 0
        self.used = 0

    def alloc(self, name, shape, dtype, top=False):
        nelem = 1
        for s_ in shape[1:]:
            nelem *= s_
        size = nelem * (2 if dtype == BF16 else 4)
        size = (size + 31) // 32 * 32
        order = range(len(self.free) - 1, -1, -1) if top else range(len(self.free))
        for i in order:
            lo, hi = self.free[i]
            if hi - lo >= size:
                if top:
                    off = hi - size
                    if lo + size == hi:
                        self.free.pop(i)
                    else:
                        self.free[i] = (lo, hi - size)
                else:
                    off = lo
                    if lo + size == hi:
                        self.free.pop(i)
                    else:
                        self.free[i] = (lo + size, hi)
                break
        else:
            raise RuntimeError("SBUF arena overflow allocating %s (%d B); free=%s" % (name, size, self.free))
        inherit = set()
        for (plo, phi, deps) in self.pending:
            if plo < off + size and off < phi:
                inherit |= deps
        t = self.nc.alloc_sbuf_tensor_at(name, list(shape), dtype, offset=off)
        self.used += size
        self.peak = max(self.peak, self.used)
        return Buf(name, off, size, t, inherit)

    def release(self, buf):
        deps = set()
        for key in buf.keys:
            st = self.res.get(key)
            if st is not None:
                deps |= set(st[0]) | set(st[1])
                del self.res[key]
        deps |= buf.inherit
        deps = self._reduce(deps)
        self.pending.append((buf.off, buf.off + buf.size, deps))
        self.used -= buf.size
        self.free.append((buf.off, buf.off + buf.size))
        self.free.sort()
        merged = []
        for lo, hi in self.free:
            if merged and merged[-1][1] == lo:
                merged[-1] = (merged[-1][0], hi)
            else:
                merged.append((lo, hi))
        self.free = merged

    @staticmethod
    def _reduce(deps):
        best = {}
        out = set()
        for d in deps:
            if d.is_dma:
                out.add(d)
            else:
                b = best.get(d.eng)
                if b is None or d.idx > b.idx:
                    best[d.eng] = d
        out |= set(best.values())
        return out

    def _state(self, key):
        st = self.res.get(key)
        if st is None:
            inh = []
            if isinstance(key, tuple) and isinstance(key[0], Buf):
                inh = list(key[0].inherit)
            st = [inh, []]
            self.res[key] = st
        return st

    def _track(self, op, r, w):
        for key in r:
            st = self._state(key)
            op.deps |= set(st[0])
            op.rawdeps |= set(st[0])
        for key in w:
            st = self._state(key)
            op.deps |= set(st[0])
            op.deps |= set(st[1])
        for key in r:
            self._state(key)[1].append(op)
        for key in w:
            st = self._state(key)
            st[0] = [op]
            st[1] = []

    def op(self, eng, fn, r=(), w=()):
        o = Op(eng, fn)
        o.idx = len(self.ops[eng])
        self._track(o, r, w)
        self.ops[eng].append(o)
        return o

    def dma(self, q, fn, r=(), w=()):
        o = Op(q, fn, is_dma=True)
        o.idx = len(self.ops[q])
        dq = self.dq[q]
        s = dq["nxt"]
        dq["nxt"] = (s + 1) % len(dq["sems"])
        if dq["last"][s] is not None:
            o.deps.add(dq["last"][s])
        dq["cnt"][s] += 1
        o.sem = dq["sems"][s]
        o.val = 16 * dq["cnt"][s]
        dq["last"][s] = o
        self._track(o, r, w)
        self.ops[q].append(o)
        return o

    def bank(self):
        i = self.bank_i
        self.bank_i = (i + 1) % 8
        return self.banks[i], ("psb", i)

    def finalize(self, final_waits):
        for e in self.ENGS:
            for o in self.ops[e]:
                need = set()
                best = {}
                for d in o.deps:
                    if d.is_dma:
                        need.add(d)
                        continue
                    if d.eng == o.eng and not o.is_dma:
                        if e == "pe":
                            continue
                    b = best.get(d.eng)
                    if b is None or d.idx > b.idx:
                        best[d.eng] = d
                need |= set(best.values())
                o.deps = need
                for d in need:
                    if not d.is_dma:
                        d.signal = True
        for d in final_waits:
            if not d.is_dma:
                d.signal = True
        for e in ("pe", "act", "dve", "pool"):
            c = 0
            for o in self.ops[e]:
                if o.is_dma:
                    continue
                if o.signal:
                    c += 1
                    o.cnt = c
        nc = self.nc
        final_waits = list(final_waits)

        def emit(e, eng):
            known = {}
            nw = 0
            for o in self.ops[e]:
                for d in o.deps:
                    if d.is_dma:
                        sem, val = d.sem, d.val
                    else:
                        sem, val = self.esem[d.eng], d.cnt
                    kk = sem.num
                    if known.get(kk, 0) >= val:
                        continue
                    known[kk] = val
                    eng.wait_ge(sem, val)
                    nw += 1
                ins = o.fn(eng)
                if o.is_dma:
                    ins.then_inc(o.sem, 16)
                elif o.signal:
                    ins.then_inc(self.esem[e], 1)
            if e == "sp":
                for d in final_waits:
                    if d.is_dma:
                        eng.wait_ge(d.sem, d.val)
                    else:
                        eng.wait_ge(self.esem[d.eng], d.cnt)
            return nw

        with nc.Block() as block:
            @block.tensor
            def _(eng):
                emit("pe", eng)

            @block.scalar
            def _(eng):
                emit("act", eng)

            @block.vector
            def _(eng):
                emit("dve", eng)

            @block.gpsimd
            def _(eng):
                emit("pool", eng)

            @block.sync
            def _(eng):
                emit("sp", eng)


def build_program(stop=None, debug_print=False):
    nc = bass.Bass("TRN2", target_bir_lowering=False)
    P = Prog(nc)

    def din(name, shape):
        return nc.dram_tensor(name, list(shape), F32, kind="ExternalInput").ap()

    xT = din("xT", [KC, 128, S])
    pT = din("pT", [2, 2, 128, S])
    cst_d = din("cst", [128, CST_N])
    wgu_d = din("wgu", [4, FC, 128, 2048])
    wdn_d = din("wdn", [4, FC, 128, 1024])
    winB_d = din("winB", [4, 2, 128, 1536])
    winV_d = din("winV", [128, 4096])
    winU_d = din("winU", [2, 128, 2048])
    wout_d = din("wout", [2, 128, 8192])
    wsT_d = din("wsT", [128, 512])
    winC_d = din("winC", [2, 128, 2048])
    winD_d = din("winD", [4, 128, 2048])
    poolw_d = din("poolw", [128, 512])
    wgate_d = din("wgate", [2, 128, 8192])
    wup_d = din("wup", [2, 128, 2048])
    outT = nc.dram_tensor("outT", [KC, 128, S], F32, kind="ExternalOutput").ap()

    def tsl(t):
        return slice(t * TW, (t + 1) * TW)

    Hb = P.alloc("H", [128, KC, S], F32)
    H = Hb.t
    cstb = P.alloc("cst", [128, CST_N], F32)
    cst = cstb.t
    onesb = P.alloc("ones", [128, 128], BF16)
    ring = [P.alloc("ring%d" % i, [128, 2048], BF16) for i in range(4)]
    ftmp = [P.alloc("ftmp%d" % i, [128, TW], F32) for i in range(4)]
    btmp = []
    cnt = {"f": 0, "b": 0, "ring": 0}

    def ft():
        b = ftmp[cnt["f"] % len(ftmp)]
        cnt["f"] += 1
        return b

    def bt():
        b = btmp[cnt["b"] % len(btmp)]
        cnt["b"] += 1
        return b

    def cs(name, a=0, b=None):
        o, w = CST[name]
        if b is None:
            b = w
        return cst[:, o + a:o + b]

    def gain_ap(ni, k):
        return cs("gains", ni * 8 + k, ni * 8 + k + 1)

    P.dma("sp", lambda e: e.dma_start(out=cst[:], in_=cst_d), w=[cstb.k()])
    xT_v = xT.rearrange("k p s -> p k s")
    outT_v = outT.rearrange("k p s -> p k s")
    for t in range(NT):
        P.dma("sp", lambda e, t=t: e.dma_start(out=H[:, :, tsl(t)], in_=xT_v[:, :, tsl(t)]),
              w=[Hb.k(k, t) for k in range(KC)])
    P.op("dve", lambda e: e.memset(onesb.t[:], 1.0), w=[onesb.k()])

    stream = []
    st = {"issued": 0, "acq": 0}

    def plan_stream():
        for li in range(2):
            for f in range(FC):
                stream.append((wgu_d[li * 2 + 0, f], 2048))
            if li == 0:
                for c in range(4):
                    for kh in range(2):
                        stream.append((winB_d[c, kh], 1536))
                for ub in range(2):
                    stream.append((winU_d[ub], 2048))
            else:
                for cb in range(2):
                    stream.append((winC_d[cb], 2048))
                for c in range(4):
                    stream.append((winD_d[c], 2048))
            for f in range(FC):
                stream.append((wgu_d[li * 2 + 1, f], 2048))

    plan_stream()

    def issue_to(n):
        n = min(n, len(stream))
        while st["issued"] < n:
            i = st["issued"]
            src, ncols = stream[i]
            rb = ring[i % 4]
            P.dma("pool", lambda e, rb=rb, src=src, ncols=ncols: e.dma_start(out=rb.t[:, 0:ncols], in_=src),
                  r=([Hb.k(0, 0)] if i < 4 else []), w=[rb.k()])
            st["issued"] += 1

    def acquire():
        i = st["acq"]
        issue_to(i + 1)
        st["acq"] += 1
        return ring[i % 4]

    def advance():
        issue_to(st["acq"] + 4)

    def direct_load(buf, dst_ap, src_ap, key):
        return P.dma("pool", lambda e: e.dma_start(out=dst_ap, in_=src_ap), w=[key])

    def norm_a(t, nsq):
        for k in range(KC):
            P.op("act", lambda e, k=k: e.activation(out=nsq.t[:, k, :], in_=H[:, k, tsl(t)], func=AF.Square),
                 r=[Hb.k(k, t)], w=[nsq.k(k)])

    def norm_b(t, ni, hnb, nsq, final=False):
        hn = None if final else hnb.t
        bk, bkey = P.bank()

        def mmn(e):
            ins = None
            for k in range(KC):
                ins = e.matmul(bk[:], onesb.t[:], nsq.t[:, k, :], start=(k == 0), stop=(k == KC - 1))
            return ins
        P.op("pe", mmn, r=[nsq.k(k) for k in range(KC)] + [onesb.k()], w=[bkey])
        sd = ft()
        P.op("act", lambda e: e.activation(out=sd.t[:], in_=bk[:], func=AF.Ln, scale=1.0 / D, bias=RMS_EPS_AP[0]),
             r=[epsb.k(0)], w=[sd.k(), bkey])
        rs = ft()
        P.op("act", lambda e: e.activation(out=rs.t[:], in_=sd.t[:], func=AF.Exp, scale=-0.5), r=[sd.k()], w=[rs.k()])
        for k in range(KC):
            if final:
                P.op("dve", lambda e, k=k: e.scalar_tensor_tensor(out=H[:, k, tsl(t)], in0=H[:, k, tsl(t)],
                                                                  scalar=gain_ap(ni, k), in1=rs.t[:],
                                                                  op0=ALU.mult, op1=ALU.mult),
                     r=[rs.k(), cstb.k()], w=[Hb.k(k, t)])
            else:
                P.op("dve", lambda e, k=k: e.scalar_tensor_tensor(out=hn[:, k, tsl(t)], in0=H[:, k, tsl(t)],
                                                                  scalar=gain_ap(ni, k), in1=rs.t[:],
                                                                  op0=ALU.mult, op1=ALU.mult),
                     r=[Hb.k(k, t), rs.k(), cstb.k()], w=[hnb.k(k, t)])

    epsb = P.alloc("eps", [128, 2], F32)
    P.op("dve", lambda e: e.memset(epsb.t[:, 0:1], RMS_EPS), w=[epsb.k(0)])
    P.op("dve", lambda e: e.memset(epsb.t[:, 1:2], LN_EPS), w=[epsb.k(1)])
    RMS_EPS_AP = [epsb.t[:, 0:1]]
    LN_EPS_AP = [epsb.t[:, 1:2]]

    def hn_keys(hnb, t):
        return [hnb.k(k, t) for k in range(KC)]

    def h_add(bk, bkey, d, t, scale):
        P.op("dve", lambda e: e.scalar_tensor_tensor(out=H[:, d, tsl(t)], in0=bk[:], scalar=float(scale),
                                                     in1=H[:, d, tsl(t)], op0=ALU.mult, op1=ALU.add),
             r=[], w=[bkey, Hb.k(d, t)])

    class NormHook:
        def __init__(self, ni, final=False, top=False):
            self.ni = ni
            self.final = final
            self.top = top
            self.hnb = None
            self.nsq = None
            self.pend = []
            self.a_done = set()

        def start(self):
            if self.ni is None:
                return
            if not self.final:
                self.hnb = P.alloc("hn", [128, KC, S], BF16, top=self.top)
            self.nsq = P.alloc("nsq", [128, KC, TW], BF16)

        def tile_done(self, t):
            if self.ni is None:
                return
            self.pend.append(t)
            if len(self.pend) > 1:
                self.flush_one()
            if len(self.pend) == 1 and self.pend[0] not in self.a_done:
                norm_a(self.pend[0], self.nsq)
                self.a_done.add(self.pend[0])

        def flush_one(self):
            if self.ni is None or not self.pend:
                return
            t = self.pend.pop(0)
            if t not in self.a_done:
                norm_a(t, self.nsq)
                self.a_done.add(t)
            norm_b(t, self.ni, self.hnb, self.nsq, self.final)
            if not self.pend and self.nsq is not None and len(self.a_done) == NT:
                P.release(self.nsq)
                self.nsq = None

        def flush(self):
            while self.ni is not None and self.pend:
                self.flush_one()

    def ffn(wi, hnb, hook, prev, pre_last=None):
        hn = hnb.t
        groups = [list(range(0, 6)), list(range(6, 12)), list(range(12, 17)), list(range(17, 22))]
        hidb = P.alloc("hid", [128, 6, S], BF16)
        wdb = P.alloc("wd", [128, 6, 1024], BF16)
        hid = hidb.t
        wd = wdb.t
        for gi, grp in enumerate(groups):
            for j, f in enumerate(grp):
                direct_load(wdb, wd[:, j, :], wdn_d[wi, f], wdb.k(j))
            if gi == len(groups) - 1 and pre_last is not None:
                pre_last()
            def step(j, t, rb):
                w = rb.t
                bg, bgk = P.bank()
                bu, buk = P.bank()
                for (bb, bbk, which) in ((bg, bgk, 0), (bu, buk, 1)):
                    def mm(e, bb=bb, which=which, t=t, w=w):
                        ins = None
                        for k in range(KC):
                            c0 = (k * 2 + which) * 128
                            ins = e.matmul(bb[:], w[:, c0:c0 + 128], hn[:, k, tsl(t)],
                                           start=(k == 0), stop=(k == KC - 1))
                        return ins
                    P.op("pe", mm, r=[rb.k()] + hn_keys(hnb, t), w=[bbk])
                sg = ft()
                P.op("act", lambda e, sg=sg, bg=bg: e.activation(out=sg.t[:], in_=bg[:], func=AF.Silu),
                     w=[sg.k(), bgk])
                P.op("dve", lambda e, sg=sg, bu=bu, j=j, t=t: e.tensor_tensor(
                    out=hid[:, j, tsl(t)], in0=sg.t[:], in1=bu[:], op=ALU.mult),
                     r=[sg.k()], w=[buk, hidb.k(j, t)])

            j0 = 0
            if gi == 0:
                rbs = [acquire(), acquire()]
                order = [(0, 0), (0, 1), (1, 0), (1, 1), (0, 2), (1, 2), (0, 3), (1, 3)]
                for si, (j, t) in enumerate(order):
                    step(j, t, rbs[j])
                    if si in (1, 2):
                        prev.flush_one()
                prev.flush()
                advance()
                j0 = 2
            for j in range(j0, len(grp)):
                rb = acquire()
                for t in range(NT):
                    step(j, t, rb)
                advance()
            last = (gi == len(groups) - 1)
            if last:
                P.release(hnb)
                hook.start()
            for t in range(NT):
                for d in range(KC):
                    bo, bok = P.bank()

                    def mm2(e, bo=bo, d=d, t=t, n=len(grp)):
                        ins = None
                        for j in range(n):
                            ins = e.matmul(bo[:], wd[:, j, d * 128:(d + 1) * 128], hid[:, j, tsl(t)],
                                           start=(j == 0), stop=(j == n - 1))
                        return ins
                    P.op("pe", mm2, r=[wdb.k(j) for j in range(len(grp))] + [hidb.k(j, t) for j in range(len(grp))],
                         w=[bok])
                    h_add(bo, bok, d, t, 0.5)
                if last:
                    hook.tile_done(t)
        P.release(hidb)
        P.release(wdb)

    def mix0_setup():
        wsb = P.alloc("wsT", [128, 4, 128], BF16, top=True)
        wsf = P.alloc("wsTf", [128, 4, 128], F32, top=True)
        P.dma("sp", lambda e: e.dma_start(out=wsf.t[:].rearrange("p a b -> p (a b)"), in_=wsT_d), w=[wsf.k()])
        for g in range(4):
            P.op("dve", lambda e, g=g: e.tensor_tensor(out=wsb.t[:, g, :], in0=wsf.t[:, g, :], in1=cs("mask"),
                                                        op=ALU.mult),
                 r=[wsf.k(), cstb.k()], w=[wsb.k(g)])
        P.release(wsf)
        return wsb

    def mix0_setup_b(wsb):
        bfull = P.alloc("bfull", [128, 4, 128], F32, top=True)
        for g in range(4):
            bk, bkey = P.bank()
            P.op("pe", lambda e, g=g, bk=bk: e.matmul(bk[:, 0:128], onesb.t[:], wsb.t[:, g, :], start=True, stop=True),
                 r=[wsb.k(g), onesb.k()], w=[bkey])
            o, _ = CST["bsb"]
            P.op("dve", lambda e, g=g, bk=bk, o=o: e.scalar_tensor_tensor(
                out=bfull.t[:, g, :], in0=bk[:, 0:128], scalar=cs("lnb", g, g + 1),
                in1=cst[:, o + g * 128:o + (g + 1) * 128], op0=ALU.mult, op1=ALU.add),
                 r=[cstb.k()], w=[bkey, bfull.k(g)])
        return bfull

    def mix0(hnb, hook, wsb, bfull, prev):
        hn = hnb.t
        vgb = P.alloc("vg", [128, 16, 512], BF16)
        vg = vgb.t
        vt4 = [P.alloc("vt4_0", [128, 4, 512], F32)]
        sq4 = P.alloc("sq4", [128, 4, 512], F32)
        wvb = wv_pre[0]
        vt4.append(P.alloc("vt4_1", [128, 4, 512], F32))
        qb = [P.alloc("q%d" % i, [128, 2 + S], F32) for i in range(2)]
        sttb = P.alloc("stt", [128, 2, 80], F32)
        AX = mybir.AxisListType.X
        def stage_a(t):
            bks = []
            for i in range(4):
                tt = 4 * t + i
                bk, bkey = P.bank()

                def mmv(e, bk=bk, tt=tt):
                    ins = None
                    for k in range(KC):
                        ins = e.matmul(bk[:], hn[:, k, tt * 128:(tt + 1) * 128], wvb.t[:, k, :],
                                       start=(k == 0), stop=(k == KC - 1))
                    return ins
                P.op("pe", mmv, r=hn_keys(hnb, t) + [wvb.k()], w=[bkey])
                bks.append((bk, bkey))
            if t == 0:
                prev.flush()
            vb = vt4[t % 2]
            sst = sttb.t[:, t % 2, :]
            sk = lambda j, t=t: sttb.k(t % 2, j)
            for i in range(4):
                bk, bkey = bks[i]
                P.op("act", lambda e, vb=vb, i=i, bk=bk: e.activation(out=vb.t[:, i, :], in_=bk[:], func=AF.Gelu),
                     w=[vb.k(i), bkey])
            for i in range(4):
                P.op("act", lambda e, vb=vb, i=i: e.activation(out=sq4.t[:, i, :], in_=vb.t[:, i, :], func=AF.Square),
                     r=[vb.k(i)], w=[sq4.k(i)])
            P.op("dve", lambda e, vb=vb, sst=sst: e.tensor_reduce(
                out=sst[:, 0:16], in_=vb.t[:].rearrange("p i (g d) -> p (i g) d", g=4), axis=AX, op=ALU.add),
                 r=[vb.k(i) for i in range(4)], w=[sk(0)])
            P.op("dve", lambda e, sst=sst: e.tensor_reduce(
                out=sst[:, 16:32], in_=sq4.t[:].rearrange("p i (g d) -> p (i g) d", g=4), axis=AX, op=ALU.add),
                 r=[sq4.k(i) for i in range(4)], w=[sk(1)])
            P.op("dve", lambda e, sst=sst: e.tensor_scalar(out=sst[:, 32:48], in0=sst[:, 0:16], scalar1=1.0 / 128,
                                                           scalar2=None, op0=ALU.mult),
                 r=[sk(0)], w=[sk(2)])
            P.op("dve", lambda e, sst=sst: e.tensor_tensor(out=sst[:, 64:80], in0=sst[:, 32:48], in1=sst[:, 32:48],
                                                           op=ALU.mult),
                 r=[sk(2)], w=[sk(4)])
            P.op("dve", lambda e, sst=sst: e.scalar_tensor_tensor(out=sst[:, 48:64], in0=sst[:, 16:32], scalar=1.0 / 128,
                                                                  in1=sst[:, 64:80], op0=ALU.mult, op1=ALU.subtract),
                 r=[sk(1), sk(4)], w=[sk(3)])

        def stage_b(t):
            vb = vt4[t % 2]
            sst = sttb.t[:, t % 2, :]
            sk = lambda j, t=t: sttb.k(t % 2, j)
            P.op("act", lambda e, sst=sst: e.activation(out=sst[:, 48:64], in_=sst[:, 48:64], func=AF.Sqrt,
                                                        bias=LN_EPS_AP[0], scale=1.0),
                 r=[epsb.k(1)], w=[sk(3)])
            P.op("dve", lambda e, sst=sst: e.reciprocal(out=sst[:, 48:64], in_=sst[:, 48:64]),
                 r=[], w=[sk(3)])
            P.op("dve", lambda e, sst=sst: e.scalar_tensor_tensor(out=sst[:, 64:80], in0=sst[:, 32:48], scalar=-1.0,
                                                                  in1=sst[:, 48:64], op0=ALU.mult, op1=ALU.mult),
                 r=[sk(2), sk(3)], w=[sk(4)])
            for i in range(4):
                tt = 4 * t + i
                for g in range(4):
                    j = i * 4 + g
                    P.op("act", lambda e, g=g, i=i, vb=vb, sst=sst, tt=tt, j=j: e.activation(
                        out=vg[:, tt, g * 128:(g + 1) * 128], in_=vb.t[:, i, g * 128:(g + 1) * 128], func=AF.Identity,
                        scale=sst[:, 48 + j:49 + j], bias=sst[:, 64 + j:65 + j]),
                         r=[vb.k(i), sk(3), sk(4)], w=[vgb.k(tt, g)])

        stage_a(0)
        stage_a(1)
        stage_b(0)
        stage_a(2)
        stage_b(1)
        stage_a(3)
        stage_b(2)
        for b_ in (vt4[0], sq4, wvb):
            P.release(b_)
        late = [lambda: stage_b(3)]

        yb_ = P.alloc("yb", [128, 4, S], BF16)
        for i in range(2):
            P.op("dve", lambda e, i=i: e.memset(qb[i].t[:, 0:2], 0.0), w=[qb[i].k("pad")])
        for c in range(4):
            r0 = acquire()
            r1 = acquire()
            q = qb[c % 2]
            for t in range(NT):
                banks = [P.bank() for _ in range(3)]
                for wi_, (bb, bbk) in enumerate(banks):
                    def mmb(e, bb=bb, wi_=wi_, t=t, r0=r0, r1=r1):
                        ins = None
                        for k in range(KC):
                            rb_ = r0 if k < 4 else r1
                            c0 = ((k % 4) * 3 + wi_) * 128
                            ins = e.matmul(bb[:], rb_.t[:, c0:c0 + 128], hn[:, k, tsl(t)],
                                           start=(k == 0), stop=(k == KC - 1))
                        return ins
                    P.op("pe", mmb, r=[r0.k(), r1.k()] + hn_keys(hnb, t), w=[bbk])
                (bcg, kcg), (bxv, kxv), (bbg, kbg) = banks
                cgs = ft()
                P.op("act", lambda e, cgs=cgs, bcg=bcg: e.activation(out=cgs.t[:], in_=bcg[:], func=AF.Copy),
                     w=[cgs.k(), kcg])
                P.op("dve", lambda e, cgs=cgs, bxv=bxv, q=q, t=t: e.tensor_tensor(
                    out=q.t[:, 2 + t * TW:2 + (t + 1) * TW], in0=cgs.t[:], in1=bxv[:], op=ALU.mult),
                     r=[cgs.k()], w=[kxv, q.k(t)])
                cv = ft()
                o, _ = CST["scw"]
                P.op("act", lambda e, cv=cv, q=q, t=t, c=c, o=o: e.activation(
                    out=cv.t[:], in_=q.t[:, 2 + t * TW:2 + (t + 1) * TW], func=AF.Copy,
                    scale=cst[:, o + c * 3 + 2:o + c * 3 + 3]),
                     r=[q.k(t), cstb.k()], w=[cv.k()])
                rq = [q.k(t), q.k("pad")] + ([q.k(t - 1)] if t > 0 else [])
                for kk, sh in ((1, 1), (0, 2)):
                    P.op("dve", lambda e, cv=cv, q=q, t=t, c=c, o=o, kk=kk, sh=sh: e.scalar_tensor_tensor(
                        out=cv.t[:], in0=q.t[:, 2 + t * TW - sh:2 + (t + 1) * TW - sh],
                        scalar=cst[:, o + c * 3 + kk:o + c * 3 + kk + 1], in1=cv.t[:], op0=ALU.mult, op1=ALU.add),
                         r=rq + [cstb.k()], w=[cv.k()])
                P.op("dve", lambda e, cv=cv, bbg=bbg, c=c, t=t: e.tensor_tensor(
                    out=yb_.t[:, c, tsl(t)], in0=cv.t[:], in1=bbg[:], op=ALU.mult),
                     r=[cv.k()], w=[kbg, yb_.k(c, t)])
                while late:
                    late.pop(0)()
                    for b_ in (vt4[1], sttb):
                        P.release(b_)
            advance()
        for i in range(2):
            P.release(qb[i])

        ya_ = P.alloc("ya", [128, 4, S], BF16)
        wob0 = load_wout(0)
        urb = [acquire(), acquire()]
        for t in range(NT):
            for g in range(4):
                if True:
                    rb = urb[g // 2]
                    half = g % 2
                    bu, buk = P.bank()

                    def mmu(e, bu=bu, half=half, t=t, rb=rb):
                        ins = None
                        for k in range(KC):
                            c0 = k * 256 + half * 128
                            ins = e.matmul(bu[:], rb.t[:, c0:c0 + 128], hn[:, k, tsl(t)],
                                           start=(k == 0), stop=(k == KC - 1))
                        return ins
                    P.op("pe", mmu, r=[rb.k()] + hn_keys(hnb, t), w=[buk])
                    bm, bmk = P.bank()

                    def mms(e, bm=bm, g=g, t=t):
                        ins = None
                        for ci in range(4):
                            tt = t * 4 + ci
                            ins = e.matmul(bm[:, ci * 128:(ci + 1) * 128], vg[:, tt, g * 128:(g + 1) * 128],
                                           wsb.t[:, g, :], start=True, stop=True)
                        return ins
                    P.op("pe", mms, r=[vgb.k(t * 4 + ci, g) for ci in range(4)] + [wsb.k(g)], w=[bmk])
                    ut = ft()
                    P.op("act", lambda e, ut=ut, bu=bu: e.activation(out=ut.t[:], in_=bu[:], func=AF.Gelu),
                         w=[ut.k(), buk])
                    mt = ft()
                    P.op("dve", lambda e, mt=mt, bm=bm, g=g: e.scalar_tensor_tensor(
                        out=mt.t[:].rearrange("p (c t) -> p c t", c=4), in0=bm[:].rearrange("p (c t) -> p c t", c=4),
                        scalar=cs("lng", g, g + 1),
                        in1=bfull.t[:, g, :].unsqueeze(1).to_broadcast([128, 4, 128]),
                        op0=ALU.mult, op1=ALU.add),
                         r=[bfull.k(g), cstb.k()], w=[bmk, mt.k()])
                    P.op("dve", lambda e, mt=mt, ut=ut, g=g, t=t: e.tensor_tensor(
                        out=ya_.t[:, g, tsl(t)], in0=mt.t[:], in1=ut.t[:], op=ALU.mult),
                         r=[mt.k(), ut.k()], w=[ya_.k(g, t)])
        advance()
        P.release(hnb)
        P.release(vgb)
        out_proj(0, [(ya_, c) for c in range(4)] + [(yb_, c) for c in range(4)], hook, wob0)
        P.release(ya_)
        P.release(yb_)

    def load_wout(li):
        halves = []
        for hf in range(2):
            b_ = P.alloc("wout%d" % hf, [128, 4, D], BF16)
            direct_load(b_, b_.t[:].rearrange("p a b -> p (a b)"), wout_d[li][:, hf * 4096:(hf + 1) * 4096], b_.k())
            halves.append(b_)
        return halves

    def out_proj(li, srcs, hook, wob=None, mid=None):
        hook.start()
        if wob is None:
            wob = load_wout(li)
        for t in range(NT):
            for d in range(KC):
                bo, bok = P.bank()

                def mmo(e, bo=bo, d=d, t=t):
                    ins = None
                    for k, (sb_, c) in enumerate(srcs):
                        ins = e.matmul(bo[:], wob[k // 4].t[:, k % 4, d * 128:(d + 1) * 128], sb_.t[:, c, tsl(t)],
                                       start=(k == 0), stop=(k == len(srcs) - 1))
                    return ins
                P.op("pe", mmo, r=[wob[0].k(), wob[1].k()] + [sb_.k(c, t) for (sb_, c) in srcs], w=[bok])
                h_add(bo, bok, d, t, 1.0)
            hook.tile_done(t)
            if mid is not None and t in mid:
                mid[t]()
        P.release(wob[0])
        P.release(wob[1])

    def mix1(hnb, hook, prev):
        hn = hnb.t
        pwb = P.alloc("poolw", [128, 4, 128], BF16, top=True)
        direct_load(pwb, pwb.t[:].rearrange("p a b -> p (a b)"), poolw_d, pwb.k())
        ycb = P.alloc("yc", [128, 4, S], BF16)
        diagb = P.alloc("diag", [128, 4, CVW, 128], BF16)
        diag_jobs = [(c, k) for c in range(4) for k in range(CVW)]
        dj = {"i": 0}

        def diag_some(n):
            o, _ = CST["cvw"]
            for _ in range(n):
                if dj["i"] >= len(diag_jobs):
                    return
                c, k = diag_jobs[dj["i"]]
                eng = "act" if dj["i"] % 2 == 0 else "dve"
                dj["i"] += 1
                sc = cst[:, o + c * CVW + k:o + c * CVW + k + 1]
                if eng == "act":
                    P.op("act", lambda e, c=c, k=k, sc=sc: e.activation(out=diagb.t[:, c, k, :], in_=cs("ident"),
                                                                        func=AF.Copy, scale=sc),
                         r=[cstb.k()], w=[diagb.k(c, k)])
                else:
                    P.op("dve", lambda e, c=c, k=k, sc=sc: e.tensor_scalar(out=diagb.t[:, c, k, :], in0=cs("ident"),
                                                                          scalar1=sc, scalar2=None, op0=ALU.mult),
                         r=[cstb.k()], w=[diagb.k(c, k)])

        plb = [P.alloc("pl%d" % i, [128, TW], BF16) for i in range(4)]
        xcb = [P.alloc("xc0", [128, XPAD + S], F32)]
        sab = [P.alloc("sa0", [128, XPAD + TW], F32)]
        P.op("dve", lambda e: e.memset(xcb[0].t[:, 0:XPAD], 0.0), w=[xcb[0].k("pad")])
        cdef = []
        plc = [0]
        for cb in range(2):
            rb = acquire()
            for half in range(2):
                gi = cb * 2 + half
                win = 2 << gi
                xc = xcb[gi % 2]
                for t in range(NT):
                    bk, bkey = P.bank()

                    def mmc(e, bk=bk, half=half, t=t, rb=rb):
                        ins = None
                        for k in range(KC):
                            c0 = k * 256 + half * 128
                            ins = e.matmul(bk[:], rb.t[:, c0:c0 + 128], hn[:, k, tsl(t)],
                                           start=(k == 0), stop=(k == KC - 1))
                        return ins
                    if cb == 0 and half == 0 and t == 1:
                        prev.flush()
                        xcb.append(P.alloc("xc1", [128, XPAD + S], F32))
                        sab.append(P.alloc("sa1", [128, XPAD + TW], F32))
                        P.op("dve", lambda e: e.memset(xcb[1].t[:, 0:XPAD], 0.0), w=[xcb[1].k("pad")])
                    P.op("pe", mmc, r=[rb.k()] + hn_keys(hnb, t), w=[bkey])
                    while len(cdef) > 1:
                        cdef.pop(0)()
                    P.op("act", lambda e, xc=xc, bk=bk, t=t: e.activation(
                        out=xc.t[:, XPAD + t * TW:XPAD + (t + 1) * TW], in_=bk[:], func=AF.Copy),
                         w=[xc.k(t), bkey])
                    W_ = XPAD + TW
                    cur = xc.t[:, t * TW:t * TW + W_]
                    rk = [xc.k(t), xc.k("pad")] + ([xc.k(t - 1)] if t > 0 else [])
                    sh = 1
                    src, srck = cur, rk
                    si = 0
                    while sh < win:
                        dst = sab[si % 2]
                        lo = 2 * sh - 1
                        P.op("dve", lambda e, dst=dst, src=src, lo=lo, sh=sh, W_=W_: e.tensor_tensor(
                            out=dst.t[:, lo:W_], in0=src[:, lo:W_], in1=src[:, lo - sh:W_ - sh], op=ALU.add),
                             r=srck, w=[dst.k()])
                        src, srck = dst.t[:, 0:W_], [dst.k()]
                        sh *= 2
                        si += 1
                    pl = plb[plc[0] % len(plb)]
                    plc[0] += 1
                    P.op("dve", lambda e, pl=pl, src=src, cur=cur, win=win: e.scalar_tensor_tensor(
                        out=pl.t[:], in0=src[:, XPAD:XPAD + TW], scalar=1.0 / win, in1=cur[:, XPAD:XPAD + TW],
                        op0=ALU.mult, op1=ALU.subtract),
                         r=srck + rk, w=[pl.k()])
                    if t == 0:
                        o, _ = CST["invc"]
                        fx = ft()
                        P.op("dve", lambda e, fx=fx, src=src, gi=gi, o=o: e.tensor_tensor(
                            out=fx.t[:, 0:16], in0=src[:, XPAD:XPAD + 16], in1=cst[:, o + gi * 16:o + (gi + 1) * 16],
                            op=ALU.mult),
                             r=srck + [cstb.k()], w=[fx.k()])
                        P.op("dve", lambda e, fx=fx, pl=pl, cur=cur: e.tensor_tensor(
                            out=pl.t[:, 0:16], in0=fx.t[:, 0:16], in1=cur[:, XPAD:XPAD + 16], op=ALU.subtract),
                             r=[fx.k()] + rk, w=[pl.k()])
                    def pool_mm(pl=pl, gi=gi, t=t):
                        b2, b2k = P.bank()
                        P.op("pe", lambda e: e.matmul(b2[:], pwb.t[:, gi, :], pl.t[:], start=True, stop=True),
                             r=[pl.k(), pwb.k()], w=[b2k])
                        P.op("act", lambda e: e.activation(
                            out=ycb.t[:, gi, tsl(t)], in_=b2[:], func=AF.Copy, scale=cs("pscale", gi, gi + 1)),
                             r=[cstb.k()], w=[b2k, ycb.k(gi, t)])
                    cdef.append(pool_mm)
                    diag_some(4)
            advance()
        for b_ in xcb + sab:
            P.release(b_)

        glub = P.alloc("glu", [128, 4, GPAD + S], BF16)
        for c in range(4):
            P.op("dve", lambda e, c=c: e.memset(glub.t[:, c, 0:GPAD], 0.0), w=[glub.k(c, "pad")])
        for c in range(4):
            rb = acquire()
            for t in range(NT):
                ba, bak = P.bank()
                bg, bgk = P.bank()
                for (bb, bbk, which) in ((ba, bak, 0), (bg, bgk, 1)):
                    def mmd(e, bb=bb, which=which, t=t, rb=rb):
                        ins = None
                        for k in range(KC):
                            c0 = k * 256 + which * 128
                            ins = e.matmul(bb[:], rb.t[:, c0:c0 + 128], hn[:, k, tsl(t)],
                                           start=(k == 0), stop=(k == KC - 1))
                        return ins
                    P.op("pe", mmd, r=[rb.k()] + hn_keys(hnb, t), w=[bbk])
                sg = ft()
                P.op("act", lambda e, sg=sg, bg=bg: e.activation(out=sg.t[:], in_=bg[:], func=AF.Sigmoid),
                     w=[sg.k(), bgk])
                P.op("dve", lambda e, sg=sg, ba=ba, c=c, t=t: e.tensor_tensor(
                    out=glub.t[:, c, GPAD + t * TW:GPAD + (t + 1) * TW], in0=sg.t[:], in1=ba[:], op=ALU.mult),
                     r=[sg.k()], w=[bak, glub.k(c, t)])
                diag_some(4)
                if cdef:
                    cdef.pop(0)()
                    if not cdef:
                        for b_ in plb:
                            P.release(b_)
                        P.release(pwb)
            advance()
        diag_some(1000)
        P.release(hnb)

        ydb = P.alloc("yd", [128, 4, S], BF16)
        hcb = P.alloc("hc", [128, 4, TW], F32)
        hib = P.alloc("hi", [128, 4, TW], BF16)
        lob = P.alloc("lo", [128, 4, TW], BF16)
        sqb = hib
        wob1 = load_wout(1)
        o_cvb, _ = CST["cvb"]

        def conv_mm(t, c):
            if True:
                bk, bkey = P.bank()

                def mmk(e, bk=bk, c=c, t=t):
                    ins = None
                    for k in range(CVW):
                        s0 = GPAD + t * TW - (CVW - 1) + k
                        ins = e.matmul(bk[:], diagb.t[:, c, k, :], glub.t[:, c, s0:s0 + TW],
                                       start=(k == 0), stop=(k == CVW - 1))
                    return ins
                rk = [glub.k(c, t), glub.k(c, "pad")] + ([glub.k(c, t - 1)] if t > 0 else [])
                P.op("pe", mmk, r=rk + [diagb.k(c, k) for k in range(CVW)], w=[bkey])
                return bk, bkey

        def conv_evac(t, c, bk, bkey):
            if True:
                P.op("act", lambda e, bk=bk, c=c: e.activation(out=hcb.t[:, c, :], in_=bk[:], func=AF.Identity,
                                                               bias=cs("cvb", c, c + 1), scale=1.0),
                     r=[cstb.k()], w=[bkey, hcb.k(c)])
                P.op("dve", lambda e, c=c: e.tensor_copy(out=hib.t[:, c, :], in_=hcb.t[:, c, :]),
                     r=[hcb.k(c)], w=[hib.k(c)])
                P.op("dve", lambda e, c=c: e.tensor_tensor(out=lob.t[:, c, :], in0=hcb.t[:, c, :], in1=hib.t[:, c, :],
                                                           op=ALU.subtract),
                     r=[hcb.k(c), hib.k(c)], w=[lob.k(c)])

        def post_mean(t):
            bm, bmk = P.bank()

            def mmm(e, bm=bm):
                ins = None
                for c in range(4):
                    ins = e.matmul(bm[:], onesb.t[:], hib.t[:, c, :], start=(c == 0), stop=False)
                    ins = e.matmul(bm[:], onesb.t[:], lob.t[:, c, :], start=False, stop=(c == 3))
                return ins
            P.op("pe", mmm, r=[hib.k(c) for c in range(4)] + [lob.k(c) for c in range(4)] + [onesb.k()], w=[bmk])
            for c in range(4):
                P.op("dve", lambda e, c=c, bm=bm: e.scalar_tensor_tensor(
                    out=hcb.t[:, c, :], in0=bm[:], scalar=-1.0 / 512, in1=hcb.t[:, c, :], op0=ALU.mult, op1=ALU.add),
                     r=[], w=[bmk, hcb.k(c)])
                P.op("act", lambda e, c=c: e.activation(out=sqb.t[:, c, :], in_=hcb.t[:, c, :], func=AF.Square),
                     r=[hcb.k(c)], w=[sqb.k(c)])

        def post_var(t):
            bv, bvk = P.bank()

            def mmv2(e, bv=bv):
                ins = None
                for c in range(4):
                    ins = e.matmul(bv[:], onesb.t[:], sqb.t[:, c, :], start=(c == 0), stop=(c == 3))
                return ins
            P.op("pe", mmv2, r=[sqb.k(c) for c in range(4)] + [onesb.k()], w=[bvk])
            sd = ft()
            P.op("act", lambda e, sd=sd, bv=bv: e.activation(out=sd.t[:], in_=bv[:], func=AF.Ln, scale=1.0 / 512,
                                                            bias=LN_EPS_AP[0]),
                 r=[epsb.k(1)], w=[sd.k(), bvk])
            rs = ft()
            P.op("act", lambda e, sd=sd, rs=rs: e.activation(out=rs.t[:], in_=sd.t[:], func=AF.Exp, scale=-0.5),
                 r=[sd.k()], w=[rs.k()])
            for c in range(4):
                P.op("dve", lambda e, c=c, rs=rs: e.tensor_tensor(out=hcb.t[:, c, :], in0=hcb.t[:, c, :], in1=rs.t[:],
                                                                  op=ALU.mult),
                     r=[rs.k()], w=[hcb.k(c)])
                P.op("act", lambda e, c=c, t=t: e.activation(
                    out=ydb.t[:, c, tsl(t)], in_=hcb.t[:, c, :], func=AF.Silu,
                    scale=cs("cvg", c, c + 1), bias=cs("cvbt", c, c + 1)),
                     r=[hcb.k(c), cstb.k()], w=[ydb.k(c, t)])

        for t in range(NT):
            bks = []
            for c in range(4):
                bks.append(conv_mm(t, c))
                if t > 0 and c == 0:
                    post_mean(t - 1)
                if t > 0 and c == 2:
                    post_var(t - 1)
            for c in range(4):
                conv_evac(t, c, bks[c][0], bks[c][1])
        for b_ in (glub, diagb):
            P.release(b_)
        out_proj(1, [(ycb, c) for c in range(4)] + [(ydb, c) for c in range(4)], hook, wob1,
                 mid={0: lambda: post_mean(NT - 1), 1: lambda: post_var(NT - 1)})
        for b_ in (hcb, hib, lob):
            P.release(b_)
        P.release(ycb)
        P.release(ydb)

    def ple_prefetch(li):
        ptb = P.alloc("pTb", [128, 2, S], BF16)
        for k in range(2):
            direct_load(ptb, ptb.t[:, k, :], pT[li, k], ptb.k(k))
        wgb = P.alloc("wgate", [128, KC, D], BF16)
        direct_load(wgb, wgb.t[:].rearrange("p a b -> p (a b)"), wgate_d[li], wgb.k())
        wub = P.alloc("wup", [128, 2, D], BF16)
        direct_load(wub, wub.t[:].rearrange("p a b -> p (a b)"), wup_d[li], wub.k())
        return ptb, wgb, wub

    def ple(li, hnb, hook, prev, bufs):
        hn = hnb.t
        ptb, wgb, wub = bufs
        hook.start()
        for t in range(NT):
            for d in range(KC):
                bg, bgk = P.bank()
                bu, buk = P.bank()

                def mmg(e, bg=bg, d=d, t=t):
                    ins = None
                    for k in range(KC):
                        ins = e.matmul(bg[:], wgb.t[:, k, d * 128:(d + 1) * 128], hn[:, k, tsl(t)],
                                       start=(k == 0), stop=(k == KC - 1))
                    return ins
                if t == 0 and d == 3:
                    prev.flush()
                P.op("pe", mmg, r=[wgb.k()] + hn_keys(hnb, t), w=[bgk])

                def mmp(e, bu=bu, d=d, t=t):
                    ins = None
                    for k in range(2):
                        ins = e.matmul(bu[:], wub.t[:, k, d * 128:(d + 1) * 128], ptb.t[:, k, tsl(t)],
                                       start=(k == 0), stop=(k == 1))
                    return ins
                P.op("pe", mmp, r=[wub.k(), ptb.k(0), ptb.k(1)], w=[buk])
                sg = ft()
                P.op("act", lambda e, sg=sg, bg=bg: e.activation(out=sg.t[:], in_=bg[:], func=AF.Sigmoid),
                     w=[sg.k(), bgk])
                P.op("dve", lambda e, sg=sg, bu=bu: e.tensor_tensor(out=sg.t[:], in0=sg.t[:], in1=bu[:], op=ALU.mult),
                     r=[], w=[sg.k(), buk])
                P.op("dve", lambda e, sg=sg, d=d, t=t: e.tensor_tensor(out=H[:, d, tsl(t)], in0=H[:, d, tsl(t)],
                                                                      in1=sg.t[:], op=ALU.add),
                     r=[sg.k()], w=[Hb.k(d, t)])
            hook.tile_done(t)
        P.release(hnb)
        P.release(ptb)
        P.release(wgb)
        P.release(wub)

    phases = []
    for li in range(2):
        phases += [("ffn", li, 0), ("mix", li), ("ffn", li, 1), ("ple", li)]
    nph = len(phases) if stop is None else stop
    def gain_idx(ph):
        if ph[0] == "ffn":
            return (0 if ph[2] == 0 else 2) * 2 + ph[1]
        if ph[0] == "mix":
            return 2 + ph[1]
        return 6 + ph[1]

    ple_bufs = {}
    wv_pre = []
    hook0 = NormHook(gain_idx(phases[0]))
    hook0.start()
    issue_to(4)
    hook0.tile_done(0)
    hook0.tile_done(1)
    hook0.flush_one()
    hook0.pend = [2, 3]
    wsb = mix0_setup()
    cur_hn = hook0.hnb
    prev = hook0
    for pi in range(nph):
        ph = phases[pi]
        nxt = gain_idx(phases[pi + 1]) if pi + 1 < nph else (8 if stop is None else None)
        hook = NormHook(nxt, final=(pi + 1 == len(phases)), top=(pi in (3, 4)))
        if ph[0] == "ffn":
            pre = None
            if ph[2] == 1 and pi + 1 < nph:
                def pre(li=ph[1]):
                    ple_bufs[li] = ple_prefetch(li)
            if ph[1] == 0 and ph[2] == 0 and pi + 1 < nph:
                def pre():
                    wvb_ = P.alloc("wv", [128, KC, 512], BF16, top=True)
                    direct_load(wvb_, wvb_.t[:].rearrange("p a b -> p (a b)"), winV_d, wvb_.k())
                    wv_pre.append(wvb_)
            ffn(ph[1] * 2 + ph[2], cur_hn, hook, prev, pre)
        elif ph[0] == "mix":
            if ph[1] == 0:
                bfull = mix0_setup_b(wsb)
                mix0(cur_hn, hook, wsb, bfull, prev)
                P.release(wsb)
                P.release(bfull)
            else:
                mix1(cur_hn, hook, prev)
        else:
            ple(ph[1], cur_hn, hook, prev, ple_bufs[ph[1]])
        cur_hn = hook.hnb
        prev = hook
    prev.flush()

    finals = []
    if stop is None:
        pass
    return nc, P, dict(H=H, Hb=Hb, outT=outT, cur_hn=cur_hn, finals=finals, tsl=tsl, norm=None)


def _finish(nc, P, ctx, stop):
    H, Hb, outT = ctx["H"], ctx["Hb"], ctx["outT"]
    finals = []
    outT_v = outT.rearrange("k p s -> p k s")
    tsl = ctx["tsl"]
    for t in range(NT):
        for hf in range(2):
            ks = [k for k in range(KC) if k % 2 == hf]
            for k in ks:
                o = P.dma("sp", lambda e, t=t, k=k: e.dma_start(out=outT_v[:, k, tsl(t)], in_=H[:, k, tsl(t)]),
                          r=[Hb.k(k, t)])
                finals.append(o)
    P.finalize(finals)
    return nc


def _prep_shared(inp):
    f = np.float32
    g = {}
    gains = np.stack([inp["ffn1_norm"][0], inp["ffn1_norm"][1], inp["mix_norm"][0], inp["mix_norm"][1],
                      inp["ffn2_norm"][0], inp["ffn2_norm"][1], inp["ple_norm"][0], inp["ple_norm"][1],
                      inp["final_norm"]], 0)
    gains = gains.reshape(9, 8, 128).transpose(2, 0, 1).reshape(128, 72)

    def col4(v):
        return np.asarray(v).reshape(4, 128).T

    cst = np.zeros((128, CST_N), f)

    def put(name, arr):
        o, w = CST[name]
        cst[:, o:o + w] = np.asarray(arr, f).reshape(128, w)

    put("gains", gains)
    put("lng", col4(inp["gm_ln_g"][0]))
    put("lnb", col4(inp["gm_ln_b"][0]))
    put("scw", inp["sc_w"][0].reshape(3, 4, 128).transpose(2, 1, 0))
    put("pscale", col4(inp["pool_scale"][0]))
    put("cvw", inp["cv_w"][0].reshape(CVW, 4, 128).transpose(2, 1, 0))
    put("cvb", col4(inp["cv_b"][0]))
    put("cvg", col4(inp["cv_ln_g"][0]))
    put("cvbt", col4(inp["cv_ln_b"][0]))
    invc = np.zeros((4, 16), f)
    for gi in range(4):
        win = 2 << gi
        for j in range(16):
            invc[gi, j] = 1.0 / min(j + 1, win)
    put("invc", np.broadcast_to(invc.reshape(1, 64), (128, 64)))
    put("ident", np.eye(128, dtype=f))
    put("mask", np.triu(np.ones((128, 128), f)))
    put("bsb", np.broadcast_to(inp["gm_b_s"][0].reshape(1, 512), (128, 512)))
    g["cst"] = cst

    wgu = np.empty((4, FC, 128, 2048), f)
    wdn = np.empty((4, FC, 128, 1024), f)
    for li in range(2):
        for wi, (kgu, kdn) in enumerate((("ffn1_w_gu", "ffn1_w_down"), ("ffn2_w_gu", "ffn2_w_down"))):
            w = inp[kgu][li].reshape(8, 128, 2, FC, 128)
            wgu[li * 2 + wi] = w.transpose(3, 1, 0, 2, 4).reshape(FC, 128, 2048)
            wdn[li * 2 + wi] = inp[kdn][li].reshape(FC, 128, 1024)
    g["wgu"] = wgu
    g["wdn"] = wdn

    def kblock(w):
        n = w.shape[1]
        return w.reshape(8, 128, n).transpose(1, 0, 2)

    win0 = inp["ab_w_in"][0]
    wB = np.empty((4, 2, 128, 1536), f)
    for c in range(4):
        cols = np.concatenate([np.arange(1536 + c * 128, 1536 + (c + 1) * 128),
                               np.arange(2048 + c * 128, 2048 + (c + 1) * 128),
                               np.arange(1024 + c * 128, 1024 + (c + 1) * 128)])
        blk = kblock(win0[:, cols])
        wB[c, 0] = blk[:, 0:4].reshape(128, 1536)
        wB[c, 1] = blk[:, 4:8].reshape(128, 1536)
    g["winB"] = wB
    g["winV"] = np.ascontiguousarray(kblock(win0[:, 512:1024]).reshape(128, 4096))
    g["winU"] = np.stack([kblock(win0[:, ub * 256:(ub + 1) * 256]).reshape(128, 2048) for ub in range(2)])
    g["wout"] = np.stack([kblock(inp["ab_w_out"][0]).reshape(128, 8192), kblock(inp["cd_w_out"][0]).reshape(128, 8192)])
    g["wsT"] = np.ascontiguousarray(inp["gm_w_s"][0].transpose(2, 0, 1).reshape(128, 512))
    win1 = inp["cd_w_in"][0]
    g["winC"] = np.stack([kblock(win1[:, cb * 256:(cb + 1) * 256]).reshape(128, 2048) for cb in range(2)])
    wD = np.empty((4, 128, 2048), f)
    for c in range(4):
        cols = np.concatenate([np.arange(512 + c * 128, 512 + (c + 1) * 128),
                               np.arange(1024 + c * 128, 1024 + (c + 1) * 128)])
        wD[c] = kblock(win1[:, cols]).reshape(128, 2048)
    g["winD"] = wD
    g["poolw"] = np.ascontiguousarray(inp["pool_w"][0].transpose(1, 0, 2).reshape(128, 512))
    g["wgate"] = np.stack([kblock(inp["ple_w_gate"][li]).reshape(128, 8192) for li in range(2)])
    g["wup"] = np.stack([inp["ple_w_up"][li].reshape(2, 128, 1024).transpose(1, 0, 2).reshape(128, 2048)
                         for li in range(2)])
    return {k: np.ascontiguousarray(v, dtype=f) for k, v in g.items()}


def _prep_core(inp, b):
    xT = np.ascontiguousarray(np.asarray(inp["x"][b]).T.reshape(KC, 128, S), dtype=np.float32)
    pT = np.ascontiguousarray(np.asarray(inp["p"][:, b]).transpose(0, 2, 1).reshape(2, 2, 128, S), dtype=np.float32)
    return {"xT": xT, "pT": pT}


_CACHE = {}


def _get_program(stop=None):
    if stop not in _CACHE:
        nc, P, ctx = build_program(stop=stop)
        _finish(nc, P, ctx, stop)
        _CACHE[stop] = (nc, P)
    return _CACHE[stop]


def run(inputs, stop=None, trace=False, ncores=NB):
    inp = {k: np.asarray(v) for k, v in inputs.items()}
    shared = _prep_shared(inp)
    in_maps = []
    for b in range(ncores):
        m = dict(shared)
        m.update(_prep_core(inp, b))
        in_maps.append(m)
    nc, P = _get_program(stop)
    res = run_bass_kernel_spmd(nc, in_maps, core_ids=list(range(ncores)), trace=trace)
    out = np.stack([np.asarray(r["outT"]).reshape(D, S).T for r in res.results], 0)
    return np.ascontiguousarray(out, dtype=np.float32), res


def kernel(**inputs):
    out, _ = run(inputs)
    return out
```

```python
import numpy as np
import concourse.bass as bass
import concourse.mybir as mybir
from concourse.bass_utils import run_bass_kernel_spmd

F32 = mybir.dt.float32
BF16 = mybir.dt.bfloat16
AF = mybir.ActivationFunctionType
ALU = mybir.AluOpType

D = 1024
S = 2048
NB = 8
KC = 8
DFF = 2816
FC = 22
NT = 4
TW = 512
PLE = 256
RMS_EPS = 1e-6
LN_EPS = 1e-5
CVW = 31
GPAD = 32
XPAD = 16

SB_BASE = 16512
SB_END = 229344

CST = {}
_off = 0
for _n, _w in [("gains", 72), ("lng", 4), ("lnb", 4), ("scw", 12), ("pscale", 4), ("cvw", 124),
               ("cvb", 4), ("cvg", 4), ("cvbt", 4), ("invc", 64), ("ident", 128), ("mask", 128),
               ("bsb", 512)]:
    CST[_n] = (_off, _w)
    _off += _w
CST_N = _off


class Op:
    __slots__ = ("eng", "fn", "deps", "rawdeps", "is_dma", "sem", "val", "idx", "signal", "cnt")

    def __init__(self, eng, fn, is_dma=False):
        self.eng = eng
        self.fn = fn
        self.deps = set()
        self.rawdeps = set()
        self.is_dma = is_dma
        self.sem = None
        self.val = None
        self.idx = None
        self.signal = False
        self.cnt = None


class Buf:
    _uid = 0

    def __init__(self, name, off, size, t, inherit):
        Buf._uid += 1
        self.uid = Buf._uid
        self.name = name
        self.off = off
        self.size = size
        self.t = t
        self.inherit = inherit
        self.keys = set()

    def k(self, *sub):
        key = (self,) + sub
        self.keys.add(key)
        return key


class Prog:
    ENGS = ("pe", "act", "dve", "pool", "sp")

    def __init__(self, nc):
        self.nc = nc
        self.ops = {e: [] for e in self.ENGS}
        self.res = {}
        self.esem = {e: nc.alloc_semaphore("sem_" + e) for e in ("pe", "act", "dve", "pool")}
        self.dq = {}
        for q, n in (("pool", 8), ("sp", 8)):
            self.dq[q] = dict(sems=[nc.alloc_semaphore("dma_%s%d" % (q, i)) for i in range(n)],
                              cnt=[0] * n, last=[None] * n, nxt=0)
        self.banks = [nc.alloc_psum_tensor("psb%d" % i, [128, 512], F32) for i in range(8)]
        self.bank_i = 0
        self.free = [(SB_BASE, SB_END)]
        self.pending = []
        self.peak = 0
        self.used = 0

    def alloc(self, name, shape, dtype, top=False):
        nelem = 1
        for s_ in shape[1:]:
            nelem *= s_
        size = nelem * (2 if dtype == BF16 else 4)
        size = (size + 31) // 32 * 32
        order = range(len(self.free) - 1, -1, -1) if top else range(len(self.free))
        for i in order:
            lo, hi = self.free[i]
            if hi - lo >= size:
                if top:
                    off = hi - size
                    if lo + size == hi:
                        self.free.pop(i)
                    else:
                        self.free[i] = (lo, hi - size)
                else:
                    off = lo
                    if lo + size == hi:
                        self.free.pop(i)
                    else:
                        self.free[i] = (lo + size, hi)
                break
        else:
            raise RuntimeError("SBUF arena overflow allocating %s (%d B); free=%s" % (name, size, self.free))
        inherit = set()
        for (plo, phi, deps) in self.pending:
            if plo < off + size and off < phi:
                inherit |= deps
        t = self.nc.alloc_sbuf_tensor_at(name, list(shape), dtype, offset=off)
        self.used += size
        self.peak = max(self.peak, self.used)
        return Buf(name, off, size, t, inherit)

    def release(self, buf):
        deps = set()
        for key in buf.keys:
            st = self.res.get(key)
            if st is not None:
                deps |= set(st[0]) | set(st[1])
                del self.res[key]
        deps |= buf.inherit
        deps = self._reduce(deps)
        self.pending.append((buf.off, buf.off + buf.size, deps))
        self.used -= buf.size
        self.free.append((buf.off, buf.off + buf.size))
        self.free.sort()
        merged = []
        for lo, hi in self.free:
            if merged and merged[-1][1] == lo:
                merged[-1] = (merged[-1][0], hi)
            else:
                merged.append((lo, hi))
        self.free = merged

    @staticmethod
    def _reduce(deps):
        best = {}
        out = set()
        for d in deps:
            if d.is_dma:
                out.add(d)
            else:
                b = best.get(d.eng)
                if b is None or d.idx > b.idx:
                    best[d.eng] = d
        out |= set(best.values())
        return out

    def _state(self, key):
        st = self.res.get(key)
        if st is None:
            inh = []
            if isinstance(key, tuple) and isinstance(key[0], Buf):
                inh = list(key[0].inherit)
            st = [inh, []]
            self.res[key] = st
        return st

    def _track(self, op, r, w):
        for key in r:
            st = self._state(key)
            op.deps |= set(st[0])
            op.rawdeps |= set(st[0])
        for key in w:
            st = self._state(key)
            op.deps |= set(st[0])
            op.deps |= set(st[1])
        for key in r:
            self._state(key)[1].append(op)
        for key in w:
            st = self._state(key)
            st[0] = [op]
            st[1] = []

    def op(self, eng, fn, r=(), w=()):
        o = Op(eng, fn)
        o.idx = len(self.ops[eng])
        self._track(o, r, w)
        self.ops[eng].append(o)
        return o

    def dma(self, q, fn, r=(), w=()):
        o = Op(q, fn, is_dma=True)
        o.idx = len(self.ops[q])
        dq = self.dq[q]
        s = dq["nxt"]
        dq["nxt"] = (s + 1) % len(dq["sems"])
        if dq["last"][s] is not None:
            o.deps.add(dq["last"][s])
        dq["cnt"][s] += 1
        o.sem = dq["sems"][s]
        o.val = 16 * dq["cnt"][s]
        dq["last"][s] = o
        self._track(o, r, w)
        self.ops[q].append(o)
        return o

    def bank(self):
        i = self.bank_i
        self.bank_i = (i + 1) % 8
        return self.banks[i], ("psb", i)

    def finalize(self, final_waits):
        for e in self.ENGS:
            for o in self.ops[e]:
                need = set()
                best = {}
                for d in o.deps:
                    if d.is_dma:
                        need.add(d)
                        continue
                    if d.eng == o.eng and not o.is_dma:
                        if e == "pe":
                            continue
                    b = best.get(d.eng)
                    if b is None or d.idx > b.idx:
                        best[d.eng] = d
                need |= set(best.values())
                o.deps = need
                for d in need:
                    if not d.is_dma:
                        d.signal = True
        for d in final_waits:
            if not d.is_dma:
                d.signal = True
        for e in ("pe", "act", "dve", "pool"):
            c = 0
            for o in self.ops[e]:
                if o.is_dma:
                    continue
                if o.signal:
                    c += 1
                    o.cnt = c
        nc = self.nc
        final_waits = list(final_waits)

        def emit(e, eng):
            known = {}
            nw = 0
            for o in self.ops[e]:
                for d in o.deps:
                    if d.is_dma:
                        sem, val = d.sem, d.val
                    else:
                        sem, val = self.esem[d.eng], d.cnt
                    kk = sem.num
                    if known.get(kk, 0) >= val:
                        continue
                    known[kk] = val
                    eng.wait_ge(sem, val)
                    nw += 1
                ins = o.fn(eng)
                if o.is_dma:
                    ins.then_inc(o.sem, 16)
                elif o.signal:
                    ins.then_inc(self.esem[e], 1)
            if e == "sp":
                for d in final_waits:
                    if d.is_dma:
                        eng.wait_ge(d.sem, d.val)
                    else:
                        eng.wait_ge(self.esem[d.eng], d.cnt)
            return nw

        with nc.Block() as block:
            @block.tensor
            def _(eng):
                emit("pe", eng)

            @block.scalar
            def _(eng):
                emit("act", eng)

            @block.vector
            def _(eng):
                emit("dve", eng)

            @block.gpsimd
            def _(eng):
                emit("pool", eng)

            @block.sync
            def _(eng):
                emit("sp", eng)


def build_program(stop=None, debug_print=False):
    nc = bass.Bass("TRN2", target_bir_lowering=False)
    P = Prog(nc)

    def din(name, shape):
        return nc.dram_tensor(name, list(shape), F32, kind="ExternalInput").ap()

    xT = din("xT", [KC, 128, S])
    pT = din("pT", [2, 2, 128, S])
    cst_d = din("cst", [128, CST_N])
    wgu_d = din("wgu", [4, FC, 128, 2048])
    wdn_d = din("wdn", [4, FC, 128, 1024])
    winB_d = din("winB", [4, 2, 128, 1536])
    winV_d = din("winV", [128, 4096])
    winU_d = din("winU", [2, 128, 2048])
    wout_d = din("wout", [2, 128, 8192])
    wsT_d = din("wsT", [128, 512])
    winC_d = din("winC", [2, 128, 2048])
    winD_d = din("winD", [4, 128, 2048])
    poolw_d = din("poolw", [128, 512])
    wgate_d = din("wgate", [2, 128, 8192])
    wup_d = din("wup", [2, 128, 2048])
    outT = nc.dram_tensor("outT", [KC, 128, S], F32, kind="ExternalOutput").ap()

    def tsl(t):
        return slice(t * TW, (t + 1) * TW)

    Hb = P.alloc("H", [128, KC, S], F32)
    H = Hb.t
    cstb = P.alloc("cst", [128, CST_N], F32)
    cst = cstb.t
    onesb = P.alloc("ones", [128, 128], BF16)
    ring = [P.alloc("ring%d" % i, [128, 2048], BF16) for i in range(4)]
    ftmp = [P.alloc("ftmp%d" % i, [128, TW], F32) for i in range(4)]
    btmp = []
    cnt = {"f": 0, "b": 0, "ring": 0}

    def ft():
        b = ftmp[cnt["f"] % len(ftmp)]
        cnt["f"] += 1
        return b

    def bt():
        b = btmp[cnt["b"] % len(btmp)]
        cnt["b"] += 1
        return b

    def cs(name, a=0, b=None):
        o, w = CST[name]
        if b is None:
            b = w
        return cst[:, o + a:o + b]

    def gain_ap(ni, k):
        return cs("gains", ni * 8 + k, ni * 8 + k + 1)

    P.dma("sp", lambda e: e.dma_start(out=cst[:], in_=cst_d), w=[cstb.k()])
    xT_v = xT.rearrange("k p s -> p k s")
    outT_v = outT.rearrange("k p s -> p k s")
    for t in range(NT):
        P.dma("sp", lambda e, t=t: e.dma_start(out=H[:, :, tsl(t)], in_=xT_v[:, :, tsl(t)]),
              w=[Hb.k(k, t) for k in range(KC)])
    P.op("dve", lambda e: e.memset(onesb.t[:], 1.0), w=[onesb.k()])

    stream = []
    st = {"issued": 0, "acq": 0}

    def plan_stream():
        for li in range(2):
            for f in range(FC):
                stream.append((wgu_d[li * 2 + 0, f], 2048))
            if li == 0:
                for c in range(4):
                    for kh in range(2):
                        stream.append((winB_d[c, kh], 1536))
                for ub in range(2):
                    stream.append((winU_d[ub], 2048))
            else:
                for cb in range(2):
                    stream.append((winC_d[cb], 2048))
                for c in range(4):
                    stream.append((winD_d[c], 2048))
            for f in range(FC):
                stream.append((wgu_d[li * 2 + 1, f], 2048))

    plan_stream()

    def issue_to(n):
        n = min(n, len(stream))
        while st["issued"] < n:
            i = st["issued"]
            src, ncols = stream[i]
            rb = ring[i % 4]
            P.dma("pool", lambda e, rb=rb, src=src, ncols=ncols: e.dma_start(out=rb.t[:, 0:ncols], in_=src),
                  r=([Hb.k(0, 0)] if i < 4 else []), w=[rb.k()])
            st["issued"] += 1

    def acquire():
        i = st["acq"]
        issue_to(i + 1)
        st["acq"] += 1
        return ring[i % 4]

    def advance():
        issue_to(st["acq"] + 4)

    def direct_load(buf, dst_ap, src_ap, key):
        return P.dma("pool", lambda e: e.dma_start(out=dst_ap, in_=src_ap), w=[key])

    def norm_a(t, nsq):
        for k in range(KC):
            P.op("act", lambda e, k=k: e.activation(out=nsq.t[:, k, :], in_=H[:, k, tsl(t)], func=AF.Square),
                 r=[Hb.k(k, t)], w=[nsq.k(k)])

    def norm_b(t, ni, hnb, nsq, final=False):
        hn = None if final else hnb.t
        bk, bkey = P.bank()

        def mmn(e):
            ins = None
            for k in range(KC):
                ins = e.matmul(bk[:], onesb.t[:], nsq.t[:, k, :], start=(k == 0), stop=(k == KC - 1))
            return ins
        P.op("pe", mmn, r=[nsq.k(k) for k in range(KC)] + [onesb.k()], w=[bkey])
        sd = ft()
        P.op("act", lambda e: e.activation(out=sd.t[:], in_=bk[:], func=AF.Ln, scale=1.0 / D, bias=RMS_EPS_AP[0]),
             r=[epsb.k(0)], w=[sd.k(), bkey])
        rs = ft()
        P.op("act", lambda e: e.activation(out=rs.t[:], in_=sd.t[:], func=AF.Exp, scale=-0.5), r=[sd.k()], w=[rs.k()])
        for k in range(KC):
            if final:
                P.op("dve", lambda e, k=k: e.scalar_tensor_tensor(out=H[:, k, tsl(t)], in0=H[:, k, tsl(t)],
                                                                  scalar=gain_ap(ni, k), in1=rs.t[:],
                                                                  op0=ALU.mult, op1=ALU.mult),
                     r=[rs.k(), cstb.k()], w=[Hb.k(k, t)])
            else:
                P.op("dve", lambda e, k=k: e.scalar_tensor_tensor(out=hn[:, k, tsl(t)], in0=H[:, k, tsl(t)],
                                                                  scalar=gain_ap(ni, k), in1=rs.t[:],
                                                                  op0=ALU.mult, op1=ALU.mult),
                     r=[Hb.k(k, t), rs.k(), cstb.k()], w=[hnb.k(k, t)])

    epsb = P.alloc("eps", [128, 2], F32)
    P.op("dve", lambda e: e.memset(epsb.t[:, 0:1], RMS_EPS), w=[epsb.k(0)])
    P.op("dve", lambda e: e.memset(epsb.t[:, 1:2], LN_EPS), w=[epsb.k(1)])
    RMS_EPS_AP = [epsb.t[:, 0:1]]
    LN_EPS_AP = [epsb.t[:, 1:2]]

    def hn_keys(hnb, t):
        return [hnb.k(k, t) for k in range(KC)]

    def h_add(bk, bkey, d, t, scale):
        P.op("dve", lambda e: e.scalar_tensor_tensor(out=H[:, d, tsl(t)], in0=bk[:], scalar=float(scale),
                                                     in1=H[:, d, tsl(t)], op0=ALU.mult, op1=ALU.add),
             r=[], w=[bkey, Hb.k(d, t)])

    class NormHook:
        def __init__(self, ni, final=False, top=False):
            self.ni = ni
            self.final = final
            self.top = top
            self.hnb = None
            self.nsq = None
            self.pend = []
            self.a_done = set()

        def start(self):
            if self.ni is None:
                return
            if not self.final:
                self.hnb = P.alloc("hn", [128, KC, S], BF16, top=self.top)
            self.nsq = P.alloc("nsq", [128, KC, TW], BF16)

        def tile_done(self, t):
            if self.ni is None:
                return
            self.pend.append(t)
            if len(self.pend) > 1:
                self.flush_one()
            if len(self.pend) == 1 and self.pend[0] not in self.a_done:
                norm_a(self.pend[0], self.nsq)
                self.a_done.add(self.pend[0])

        def flush_one(self):
            if self.ni is None or not self.pend:
                return
            t = self.pend.pop(0)
            if t not in self.a_done:
                norm_a(t, self.nsq)
                self.a_done.add(t)
            norm_b(t, self.ni, self.hnb, self.nsq, self.final)
            if not self.pend and self.nsq is not None and len(self.a_done) == NT:
                P.release(self.nsq)
                self.nsq = None

        def flush(self):
            while self.ni is not None and self.pend:
                self.flush_one()

    def ffn(wi, hnb, hook, prev, pre_last=None):
        hn = hnb.t
        groups = [list(range(0, 6)), list(range(6, 12)), list(range(12, 17)), list(range(17, 22))]
        hidb = P.alloc("hid", [128, 6, S], BF16)
        wdb = P.alloc("wd", [128, 6, 1024], BF16)
        hid = hidb.t
        wd = wdb.t
        for gi, grp in enumerate(groups):
            for j, f in enumerate(grp):
                direct_load(wdb, wd[:, j, :], wdn_d[wi, f], wdb.k(j))
            if gi == len(groups) - 1 and pre_last is not None:
                pre_last()
            def step(j, t, rb):
                w = rb.t
                bg, bgk = P.bank()
                bu, buk = P.bank()
                for (bb, bbk, which) in ((bg, bgk, 0), (bu, buk, 1)):
                    def mm(e, bb=bb, which=which, t=t, w=w):
                        ins = None
                        for k in range(KC):
                            c0 = (k * 2 + which) * 128
                            ins = e.matmul(bb[:], w[:, c0:c0 + 128], hn[:, k, tsl(t)],
                                           start=(k == 0), stop=(k == KC - 1))
                        return ins
                    P.op("pe", mm, r=[rb.k()] + hn_keys(hnb, t), w=[bbk])
                sg = ft()
                P.op("act", lambda e, sg=sg, bg=bg: e.activation(out=sg.t[:], in_=bg[:], func=AF.Silu),
                     w=[sg.k(), bgk])
                P.op("dve", lambda e, sg=sg, bu=bu, j=j, t=t: e.tensor_tensor(
                    out=hid[:, j, tsl(t)], in0=sg.t[:], in1=bu[:], op=ALU.mult),
                     r=[sg.k()], w=[buk, hidb.k(j, t)])

            j0 = 0
            if gi == 0:
                rbs = [acquire(), acquire()]
                order = [(0, 0), (0, 1), (1, 0), (1, 1), (0, 2), (1, 2), (0, 3), (1, 3)]
                for si, (j, t) in enumerate(order):
                    step(j, t, rbs[j])
                    if si in (1, 2):
                        prev.flush_one()
                prev.flush()
                advance()
                j0 = 2
            for j in range(j0, len(grp)):
                rb = acquire()
                for t in range(NT):
                    step(j, t, rb)
                advance()
            last = (gi == len(groups) - 1)
            if last:
                P.release(hnb)
                hook.start()
            for t in range(NT):
                for d in range(KC):
                    bo, bok = P.bank()

                    def mm2(e, bo=bo, d=d, t=t, n=len(grp)):
                        ins = None
                        for j in range(n):
                            ins = e.matmul(bo[:], wd[:, j, d * 128:(d + 1) * 128], hid[:, j, tsl(t)],
                                           start=(j == 0), stop=(j == n - 1))
                        return ins
                    P.op("pe", mm2, r=[wdb.k(j) for j in range(len(grp))] + [hidb.k(j, t) for j in range(len(grp))],
                         w=[bok])
                    h_add(bo, bok, d, t, 0.5)
                if last:
                    hook.tile_done(t)
        P.release(hidb)
        P.release(wdb)

    def mix0_setup():
        wsb = P.alloc("wsT", [128, 4, 128], BF16, top=True)
        wsf = P.alloc("wsTf", [128, 4, 128], F32, top=True)
        P.dma("sp", lambda e: e.dma_start(out=wsf.t[:].rearrange("p a b -> p (a b)"), in_=wsT_d), w=[wsf.k()])
        for g in range(4):
            P.op("dve", lambda e, g=g: e.tensor_tensor(out=wsb.t[:, g, :], in0=wsf.t[:, g, :], in1=cs("mask"),
                                                        op=ALU.mult),
                 r=[wsf.k(), cstb.k()], w=[wsb.k(g)])
        P.release(wsf)
        return wsb

    def mix0_setup_b(wsb):
        bfull = P.alloc("bfull", [128, 4, 128], F32, top=True)
        for g in range(4):
            bk, bkey = P.bank()
            P.op("pe", lambda e, g=g, bk=bk: e.matmul(bk[:, 0:128], onesb.t[:], wsb.t[:, g, :], start=True, stop=True),
                 r=[wsb.k(g), onesb.k()], w=[bkey])
            o, _ = CST["bsb"]
            P.op("dve", lambda e, g=g, bk=bk, o=o: e.scalar_tensor_tensor(
                out=bfull.t[:, g, :], in0=bk[:, 0:128], scalar=cs("lnb", g, g + 1),
                in1=cst[:, o + g * 128:o + (g + 1) * 128], op0=ALU.mult, op1=ALU.add),
                 r=[cstb.k()], w=[bkey, bfull.k(g)])
        return bfull

    def mix0(hnb, hook, wsb, bfull, prev):
        hn = hnb.t
        vgb = P.alloc("vg", [128, 16, 512], BF16)
        vg = vgb.t
        vt4 = [P.alloc("vt4_0", [128, 4, 512], F32)]
        sq4 = P.alloc("sq4", [128, 4, 512], F32)
        wvb = wv_pre[0]
        vt4.append(P.alloc("vt4_1", [128, 4, 512], F32))
        qb = [P.alloc("q%d" % i, [128, 2 + S], F32) for i in range(2)]
        sttb = P.alloc("stt", [128, 2, 80], F32)
        AX = mybir.AxisListType.X
        def stage_a(t):
            bks = []
            for i in range(4):
                tt = 4 * t + i
                bk, bkey = P.bank()

                def mmv(e, bk=bk, tt=tt):
                    ins = None
                    for k in range(KC):
                        ins = e.matmul(bk[:], hn[:, k, tt * 128:(tt + 1) * 128], wvb.t[:, k, :],
                                       start=(k == 0), stop=(k == KC - 1))
                    return ins
                P.op("pe", mmv, r=hn_keys(hnb, t) + [wvb.k()], w=[bkey])
                bks.append((bk, bkey))
            if t == 0:
                prev.flush()
            vb = vt4[t % 2]
            sst = sttb.t[:, t % 2, :]
            sk = lambda j, t=t: sttb.k(t % 2, j)
            for i in range(4):
                bk, bkey = bks[i]
                P.op("act", lambda e, vb=vb, i=i, bk=bk: e.activation(out=vb.t[:, i, :], in_=bk[:], func=AF.Gelu),
                     w=[vb.k(i), bkey])
            for i in range(4):
                P.op("act", lambda e, vb=vb, i=i: e.activation(out=sq4.t[:, i, :], in_=vb.t[:, i, :], func=AF.Square),
                     r=[vb.k(i)], w=[sq4.k(i)])
            P.op("dve", lambda e, vb=vb, sst=sst: e.tensor_reduce(
                out=sst[:, 0:16], in_=vb.t[:].rearrange("p i (g d) -> p (i g) d", g=4), axis=AX, op=ALU.add),
                 r=[vb.k(i) for i in range(4)], w=[sk(0)])
            P.op("dve", lambda e, sst=sst: e.tensor_reduce(
                out=sst[:, 16:32], in_=sq4.t[:].rearrange("p i (g d) -> p (i g) d", g=4), axis=AX, op=ALU.add),
                 r=[sq4.k(i) for i in range(4)], w=[sk(1)])
            P.op("dve", lambda e, sst=sst: e.tensor_scalar(out=sst[:, 32:48], in0=sst[:, 0:16], scalar1=1.0 / 128,
                                                           scalar2=None, op0=ALU.mult),
                 r=[sk(0)], w=[sk(2)])
            P.op("dve", lambda e, sst=sst: e.tensor_tensor(out=sst[:, 64:80], in0=sst[:, 32:48], in1=sst[:, 32:48],
                                                           op=ALU.mult),
                 r=[sk(2)], w=[sk(4)])
            P.op("dve", lambda e, sst=sst: e.scalar_tensor_tensor(out=sst[:, 48:64], in0=sst[:, 16:32], scalar=1.0 / 128,
                                                                  in1=sst[:, 64:80], op0=ALU.mult, op1=ALU.subtract),
                 r=[sk(1), sk(4)], w=[sk(3)])

        def stage_b(t):
            vb = vt4[t % 2]
            sst = sttb.t[:, t % 2, :]
            sk = lambda j, t=t: sttb.k(t % 2, j)
            P.op("act", lambda e, sst=sst: e.activation(out=sst[:, 48:64], in_=sst[:, 48:64], func=AF.Sqrt,
                                                        bias=LN_EPS_AP[0], scale=1.0),
                 r=[epsb.k(1)], w=[sk(3)])
            P.op("dve", lambda e, sst=sst: e.reciprocal(out=sst[:, 48:64], in_=sst[:, 48:64]),
                 r=[], w=[sk(3)])
            P.op("dve", lambda e, sst=sst: e.scalar_tensor_tensor(out=sst[:, 64:80], in0=sst[:, 32:48], scalar=-1.0,
                                                                  in1=sst[:, 48:64], op0=ALU.mult, op1=ALU.mult),
                 r=[sk(2), sk(3)], w=[sk(4)])
            for i in range(2):
                tt = 4 * t + i
                for g in range(4):
                    j = i * 4 + g
                    P.op("act", lambda e, g=g, i=i, vb=vb, sst=sst, tt=tt, j=j: e.activation(
                        out=vg[:, tt, g * 128:(g + 1) * 128], in_=vb.t[:, i, g * 128:(g + 1) * 128], func=AF.Identity,
                        scale=sst[:, 48 + j:49 + j], bias=sst[:, 64 + j:65 + j]),
                         r=[vb.k(i), sk(3), sk(4)], w=[vgb.k(tt, g)])
            vv = vb.t[:, 2:4, :].rearrange("p i (g d) -> p (i g) d", g=4)
            P.op("dve", lambda e, vv=vv, sst=sst: e.tensor_tensor(
                out=vv, in0=vv, in1=sst[:, 56:64].unsqueeze(2).to_broadcast([128, 8, 128]), op=ALU.mult),
                 r=[sk(3)], w=[vb.k(2), vb.k(3)])
            P.op("dve", lambda e, vv=vv, sst=sst, t=t: e.tensor_tensor(
                out=vg[:, 4 * t + 2:4 * t + 4, :].rearrange("p i (g d) -> p (i g) d", g=4), in0=vv,
                in1=sst[:, 72:80].unsqueeze(2).to_broadcast([128, 8, 128]), op=ALU.add),
                 r=[vb.k(2), vb.k(3), sk(4)], w=[vgb.k(4 * t + i, g) for i in (2, 3) for g in range(4)])

        stage_a(0)
        stage_a(1)
        stage_b(0)
        stage_a(2)
        stage_b(1)
        stage_a(3)
        stage_b(2)
        for b_ in (vt4[0], sq4, wvb):
            P.release(b_)
        late = [lambda: stage_b(3)]

        yb_ = P.alloc("yb", [128, 4, S], BF16)
        for i in range(2):
            P.op("dve", lambda e, i=i: e.memset(qb[i].t[:, 0:2], 0.0), w=[qb[i].k("pad")])
        for c in range(4):
            r0 = acquire()
            r1 = acquire()
            q = qb[c % 2]
            for t in range(NT):
                banks = [P.bank() for _ in range(3)]
                for wi_, (bb, bbk) in enumerate(banks):
                    def mmb(e, bb=bb, wi_=wi_, t=t, r0=r0, r1=r1):
                        ins = None
                        for k in range(KC):
                            rb_ = r0 if k < 4 else r1
                            c0 = ((k % 4) * 3 + wi_) * 128
                            ins = e.matmul(bb[:], rb_.t[:, c0:c0 + 128], hn[:, k, tsl(t)],
                                           start=(k == 0), stop=(k == KC - 1))
                        return ins
                    P.op("pe", mmb, r=[r0.k(), r1.k()] + hn_keys(hnb, t), w=[bbk])
                (bcg, kcg), (bxv, kxv), (bbg, kbg) = banks
                cgs = ft()
                P.op("act", lambda e, cgs=cgs, bcg=bcg: e.activation(out=cgs.t[:], in_=bcg[:], func=AF.Copy),
                     w=[cgs.k(), kcg])
                P.op("dve", lambda e, cgs=cgs, bxv=bxv, q=q, t=t: e.tensor_tensor(
                    out=q.t[:, 2 + t * TW:2 + (t + 1) * TW], in0=cgs.t[:], in1=bxv[:], op=ALU.mult),
                     r=[cgs.k()], w=[kxv, q.k(t)])
                cv = ft()
                o, _ = CST["scw"]
                P.op("act", lambda e, cv=cv, q=q, t=t, c=c, o=o: e.activation(
                    out=cv.t[:], in_=q.t[:, 2 + t * TW:2 + (t + 1) * TW], func=AF.Copy,
                    scale=cst[:, o + c * 3 + 2:o + c * 3 + 3]),
                     r=[q.k(t), cstb.k()], w=[cv.k()])
                rq = [q.k(t), q.k("pad")] + ([q.k(t - 1)] if t > 0 else [])
                for kk, sh in ((1, 1), (0, 2)):
                    P.op("dve", lambda e, cv=cv, q=q, t=t, c=c, o=o, kk=kk, sh=sh: e.scalar_tensor_tensor(
                        out=cv.t[:], in0=q.t[:, 2 + t * TW - sh:2 + (t + 1) * TW - sh],
                        scalar=cst[:, o + c * 3 + kk:o + c * 3 + kk + 1], in1=cv.t[:], op0=ALU.mult, op1=ALU.add),
                         r=rq + [cstb.k()], w=[cv.k()])
                P.op("dve", lambda e, cv=cv, bbg=bbg, c=c, t=t: e.tensor_tensor(
                    out=yb_.t[:, c, tsl(t)], in0=cv.t[:], in1=bbg[:], op=ALU.mult),
                     r=[cv.k()], w=[kbg, yb_.k(c, t)])
                while late:
                    late.pop(0)()
                    for b_ in (vt4[1], sttb):
                        P.release(b_)
            advance()
        for i in range(2):
            P.release(qb[i])

        ya_ = P.alloc("ya", [128, 4, S], BF16)
        wob0 = load_wout(0)
        urb = [acquire(), acquire()]
        for t in range(NT):
            for g in range(4):
                if True:
                    rb = urb[g // 2]
                    half = g % 2
                    bu, buk = P.bank()

                    def mmu(e, bu=bu, half=half, t=t, rb=rb):
                        ins = None
                        for k in range(KC):
                            c0 = k * 256 + half * 128
                            ins = e.matmul(bu[:], rb.t[:, c0:c0 + 128], hn[:, k, tsl(t)],
                                           start=(k == 0), stop=(k == KC - 1))
                        return ins
                    P.op("pe", mmu, r=[rb.k()] + hn_keys(hnb, t), w=[buk])
                    bm, bmk = P.bank()

                    def mms(e, bm=bm, g=g, t=t):
                        ins = None
                        for ci in range(4):
                            tt = t * 4 + ci
                            ins = e.matmul(bm[:, ci * 128:(ci + 1) * 128], vg[:, tt, g * 128:(g + 1) * 128],
                                           wsb.t[:, g, :], start=True, stop=True)
                        return ins
                    P.op("pe", mms, r=[vgb.k(t * 4 + ci, g) for ci in range(4)] + [wsb.k(g)], w=[bmk])
                    ut = ft()
                    P.op("act", lambda e, ut=ut, bu=bu: e.activation(out=ut.t[:], in_=bu[:], func=AF.Gelu),
                         w=[ut.k(), buk])
                    mt = ft()
                    P.op("dve", lambda e, mt=mt, bm=bm, g=g: e.scalar_tensor_tensor(
                        out=mt.t[:].rearrange("p (c t) -> p c t", c=4), in0=bm[:].rearrange("p (c t) -> p c t", c=4),
                        scalar=cs("lng", g, g + 1),
                        in1=bfull.t[:, g, :].unsqueeze(1).to_broadcast([128, 4, 128]),
                        op0=ALU.mult, op1=ALU.add),
                         r=[bfull.k(g), cstb.k()], w=[bmk, mt.k()])
                    P.op("dve", lambda e, mt=mt, ut=ut, g=g, t=t: e.tensor_tensor(
                        out=ya_.t[:, g, tsl(t)], in0=mt.t[:], in1=ut.t[:], op=ALU.mult),
                         r=[mt.k(), ut.k()], w=[ya_.k(g, t)])
        advance()
        P.release(hnb)
        P.release(vgb)
        out_proj(0, [(ya_, c) for c in range(4)] + [(yb_, c) for c in range(4)], hook, wob0)
        P.release(ya_)
        P.release(yb_)

    def load_wout(li):
        halves = []
        for hf in range(2):
            b_ = P.alloc("wout%d" % hf, [128, 4, D], BF16)
            direct_load(b_, b_.t[:].rearrange("p a b -> p (a b)"), wout_d[li][:, hf * 4096:(hf + 1) * 4096], b_.k())
            halves.append(b_)
        return halves

    def out_proj(li, srcs, hook, wob=None, mid=None):
        hook.start()
        if wob is None:
            wob = load_wout(li)
        for t in range(NT):
            for d in range(KC):
                bo, bok = P.bank()

                def mmo(e, bo=bo, d=d, t=t):
                    ins = None
                    for k, (sb_, c) in enumerate(srcs):
                        ins = e.matmul(bo[:], wob[k // 4].t[:, k % 4, d * 128:(d + 1) * 128], sb_.t[:, c, tsl(t)],
                                       start=(k == 0), stop=(k == len(srcs) - 1))
                    return ins
                P.op("pe", mmo, r=[wob[0].k(), wob[1].k()] + [sb_.k(c, t) for (sb_, c) in srcs], w=[bok])
                h_add(bo, bok, d, t, 1.0)
            hook.tile_done(t)
            if mid is not None and t in mid:
                mid[t]()
        P.release(wob[0])
        P.release(wob[1])

    def mix1(hnb, hook, prev):
        hn = hnb.t
        pwb = P.alloc("poolw", [128, 4, 128], BF16, top=True)
        direct_load(pwb, pwb.t[:].rearrange("p a b -> p (a b)"), poolw_d, pwb.k())
        ycb = P.alloc("yc", [128, 4, S], BF16)
        diagb = P.alloc("diag", [128, 4, CVW, 128], BF16)
        diag_jobs = [(c, k) for c in range(4) for k in range(CVW)]
        dj = {"i": 0}

        def diag_some(n):
            o, _ = CST["cvw"]
            for _ in range(n):
                if dj["i"] >= len(diag_jobs):
                    return
                c, k = diag_jobs[dj["i"]]
                eng = "act" if dj["i"] % 2 == 0 else "dve"
                dj["i"] += 1
                sc = cst[:, o + c * CVW + k:o + c * CVW + k + 1]
                if eng == "act":
                    P.op("act", lambda e, c=c, k=k, sc=sc: e.activation(out=diagb.t[:, c, k, :], in_=cs("ident"),
                                                                        func=AF.Copy, scale=sc),
                         r=[cstb.k()], w=[diagb.k(c, k)])
                else:
                    P.op("dve", lambda e, c=c, k=k, sc=sc: e.tensor_scalar(out=diagb.t[:, c, k, :], in0=cs("ident"),
                                                                          scalar1=sc, scalar2=None, op0=ALU.mult),
                         r=[cstb.k()], w=[diagb.k(c, k)])

        plb = [P.alloc("pl%d" % i, [128, TW], BF16) for i in range(4)]
        xcb = [P.alloc("xc0", [128, XPAD + S], F32)]
        sab = [P.alloc("sa0", [128, XPAD + TW], F32)]
        P.op("dve", lambda e: e.memset(xcb[0].t[:, 0:XPAD], 0.0), w=[xcb[0].k("pad")])
        cdef = []
        plc = [0]
        for cb in range(2):
            rb = acquire()
            for half in range(2):
                gi = cb * 2 + half
                win = 2 << gi
                xc = xcb[gi % 2]
                for t in range(NT):
                    bk, bkey = P.bank()

                    def mmc(e, bk=bk, half=half, t=t, rb=rb):
                        ins = None
                        for k in range(KC):
                            c0 = k * 256 + half * 128
                            ins = e.matmul(bk[:], rb.t[:, c0:c0 + 128], hn[:, k, tsl(t)],
                                           start=(k == 0), stop=(k == KC - 1))
                        return ins
                    if cb == 0 and half == 0 and t == 1:
                        prev.flush()
                        xcb.append(P.alloc("xc1", [128, XPAD + S], F32))
                        sab.append(P.alloc("sa1", [128, XPAD + TW], F32))
                        P.op("dve", lambda e: e.memset(xcb[1].t[:, 0:XPAD], 0.0), w=[xcb[1].k("pad")])
                    P.op("pe", mmc, r=[rb.k()] + hn_keys(hnb, t), w=[bkey])
                    while len(cdef) > 1:
                        cdef.pop(0)()
                    P.op("act", lambda e, xc=xc, bk=bk, t=t: e.activation(
                        out=xc.t[:, XPAD + t * TW:XPAD + (t + 1) * TW], in_=bk[:], func=AF.Copy),
                         w=[xc.k(t), bkey])
                    W_ = XPAD + TW
                    cur = xc.t[:, t * TW:t * TW + W_]
                    rk = [xc.k(t), xc.k("pad")] + ([xc.k(t - 1)] if t > 0 else [])
                    sh = 1
                    src, srck = cur, rk
                    si = 0
                    while sh < win:
                        dst = sab[si % 2]
                        lo = 2 * sh - 1
                        P.op("dve", lambda e, dst=dst, src=src, lo=lo, sh=sh, W_=W_: e.tensor_tensor(
                            out=dst.t[:, lo:W_], in0=src[:, lo:W_], in1=src[:, lo - sh:W_ - sh], op=ALU.add),
                             r=srck, w=[dst.k()])
                        src, srck = dst.t[:, 0:W_], [dst.k()]
                        sh *= 2
                        si += 1
                    pl = plb[plc[0] % len(plb)]
                    plc[0] += 1
                    P.op("dve", lambda e, pl=pl, src=src, cur=cur, win=win: e.scalar_tensor_tensor(
                        out=pl.t[:], in0=src[:, XPAD:XPAD + TW], scalar=1.0 / win, in1=cur[:, XPAD:XPAD + TW],
                        op0=ALU.mult, op1=ALU.subtract),
                         r=srck + rk, w=[pl.k()])
                    if t == 0:
                        o, _ = CST["invc"]
                        fx = ft()
                        P.op("dve", lambda e, fx=fx, src=src, gi=gi, o=o: e.tensor_tensor(
                            out=fx.t[:, 0:16], in0=src[:, XPAD:XPAD + 16], in1=cst[:, o + gi * 16:o + (gi + 1) * 16],
                            op=ALU.mult),
                             r=srck + [cstb.k()], w=[fx.k()])
                        P.op("dve", lambda e, fx=fx, pl=pl, cur=cur: e.tensor_tensor(
                            out=pl.t[:, 0:16], in0=fx.t[:, 0:16], in1=cur[:, XPAD:XPAD + 16], op=ALU.subtract),
                             r=[fx.k()] + rk, w=[pl.k()])
                    def pool_mm(pl=pl, gi=gi, t=t):
                        b2, b2k = P.bank()
                        P.op("pe", lambda e: e.matmul(b2[:], pwb.t[:, gi, :], pl.t[:], start=True, stop=True),
                             r=[pl.k(), pwb.k()], w=[b2k])
                        P.op("act", lambda e: e.activation(
                            out=ycb.t[:, gi, tsl(t)], in_=b2[:], func=AF.Copy, scale=cs("pscale", gi, gi + 1)),
                             r=[cstb.k()], w=[b2k, ycb.k(gi, t)])
                    cdef.append(pool_mm)
                    diag_some(4)
            advance()
        for b_ in xcb + sab:
            P.release(b_)

        glub = P.alloc("glu", [128, 4, GPAD + S], BF16)
        for c in range(4):
            P.op("dve", lambda e, c=c: e.memset(glub.t[:, c, 0:GPAD], 0.0), w=[glub.k(c, "pad")])
        for c in range(4):
            rb = acquire()
            for t in range(NT):
                ba, bak = P.bank()
                bg, bgk = P.bank()
                for (bb, bbk, which) in ((ba, bak, 0), (bg, bgk, 1)):
                    def mmd(e, bb=bb, which=which, t=t, rb=rb):
                        ins = None
                        for k in range(KC):
                            c0 = k * 256 + which * 128
                            ins = e.matmul(bb[:], rb.t[:, c0:c0 + 128], hn[:, k, tsl(t)],
                                           start=(k == 0), stop=(k == KC - 1))
                        return ins
                    P.op("pe", mmd, r=[rb.k()] + hn_keys(hnb, t), w=[bbk])
                sg = ft()
                P.op("act", lambda e, sg=sg, bg=bg: e.activation(out=sg.t[:], in_=bg[:], func=AF.Sigmoid),
                     w=[sg.k(), bgk])
                P.op("dve", lambda e, sg=sg, ba=ba, c=c, t=t: e.tensor_tensor(
                    out=glub.t[:, c, GPAD + t * TW:GPAD + (t + 1) * TW], in0=sg.t[:], in1=ba[:], op=ALU.mult),
                     r=[sg.k()], w=[bak, glub.k(c, t)])
                diag_some(4)
                if cdef:
                    cdef.pop(0)()
                    if not cdef:
                        for b_ in plb:
                            P.release(b_)
                        P.release(pwb)
            advance()
        diag_some(1000)
        P.release(hnb)

        ydb = P.alloc("yd", [128, 4, S], BF16)
        hcb = P.alloc("hc", [128, 4, TW], F32)
        hib = P.alloc("hi", [128, 4, TW], BF16)
        lob = P.alloc("lo", [128, 4, TW], BF16)
        sqb = hib
        wob1 = load_wout(1)
        o_cvb, _ = CST["cvb"]

        def conv_mm(t, c):
            if True:
                bk, bkey = P.bank()

                def mmk(e, bk=bk, c=c, t=t):
                    ins = None
                    for k in range(CVW):
                        s0 = GPAD + t * TW - (CVW - 1) + k
                        ins = e.matmul(bk[:], diagb.t[:, c, k, :], glub.t[:, c, s0:s0 + TW],
                                       start=(k == 0), stop=(k == CVW - 1))
                    return ins
                rk = [glub.k(c, t), glub.k(c, "pad")] + ([glub.k(c, t - 1)] if t > 0 else [])
                P.op("pe", mmk, r=rk + [diagb.k(c, k) for k in range(CVW)], w=[bkey])
                return bk, bkey

        def conv_evac(t, c, bk, bkey):
            if True:
                P.op("act", lambda e, bk=bk, c=c: e.activation(out=hcb.t[:, c, :], in_=bk[:], func=AF.Identity,
                                                               bias=cs("cvb", c, c + 1), scale=1.0),
                     r=[cstb.k()], w=[bkey, hcb.k(c)])
                P.op("dve", lambda e, c=c: e.tensor_copy(out=hib.t[:, c, :], in_=hcb.t[:, c, :]),
                     r=[hcb.k(c)], w=[hib.k(c)])
                P.op("dve", lambda e, c=c: e.tensor_tensor(out=lob.t[:, c, :], in0=hcb.t[:, c, :], in1=hib.t[:, c, :],
                                                           op=ALU.subtract),
                     r=[hcb.k(c), hib.k(c)], w=[lob.k(c)])

        def post_mean(t):
            bm, bmk = P.bank()

            def mmm(e, bm=bm):
                ins = None
                for c in range(4):
                    ins = e.matmul(bm[:], onesb.t[:], hib.t[:, c, :], start=(c == 0), stop=False)
                    ins = e.matmul(bm[:], onesb.t[:], lob.t[:, c, :], start=False, stop=(c == 3))
                return ins
            P.op("pe", mmm, r=[hib.k(c) for c in range(4)] + [lob.k(c) for c in range(4)] + [onesb.k()], w=[bmk])
            for c in range(4):
                P.op("dve", lambda e, c=c, bm=bm: e.scalar_tensor_tensor(
                    out=hcb.t[:, c, :], in0=bm[:], scalar=-1.0 / 512, in1=hcb.t[:, c, :], op0=ALU.mult, op1=ALU.add),
                     r=[], w=[bmk, hcb.k(c)])
                P.op("act", lambda e, c=c: e.activation(out=sqb.t[:, c, :], in_=hcb.t[:, c, :], func=AF.Square),
                     r=[hcb.k(c)], w=[sqb.k(c)])

        def post_var(t):
            bv, bvk = P.bank()

            def mmv2(e, bv=bv):
                ins = None
                for c in range(4):
                    ins = e.matmul(bv[:], onesb.t[:], sqb.t[:, c, :], start=(c == 0), stop=(c == 3))
                return ins
            P.op("pe", mmv2, r=[sqb.k(c) for c in range(4)] + [onesb.k()], w=[bvk])
            sd = ft()
            P.op("act", lambda e, sd=sd, bv=bv: e.activation(out=sd.t[:], in_=bv[:], func=AF.Ln, scale=1.0 / 512,
                                                            bias=LN_EPS_AP[0]),
                 r=[epsb.k(1)], w=[sd.k(), bvk])
            rs = ft()
            P.op("act", lambda e, sd=sd, rs=rs: e.activation(out=rs.t[:], in_=sd.t[:], func=AF.Exp, scale=-0.5),
                 r=[sd.k()], w=[rs.k()])
            for c in range(4):
                P.op("dve", lambda e, c=c, rs=rs: e.tensor_tensor(out=hcb.t[:, c, :], in0=hcb.t[:, c, :], in1=rs.t[:],
                                                                  op=ALU.mult),
                     r=[rs.k()], w=[hcb.k(c)])
                P.op("act", lambda e, c=c, t=t: e.activation(
                    out=ydb.t[:, c, tsl(t)], in_=hcb.t[:, c, :], func=AF.Silu,
                    scale=cs("cvg", c, c + 1), bias=cs("cvbt", c, c + 1)),
                     r=[hcb.k(c), cstb.k()], w=[ydb.k(c, t)])

        for t in range(NT):
            bks = []
            for c in range(4):
                bks.append(conv_mm(t, c))
                if t > 0 and c == 0:
                    post_mean(t - 1)
                if t > 0 and c == 2:
                    post_var(t - 1)
            for c in range(4):
                conv_evac(t, c, bks[c][0], bks[c][1])
        for b_ in (glub, diagb):
            P.release(b_)
        out_proj(1, [(ycb, c) for c in range(4)] + [(ydb, c) for c in range(4)], hook, wob1,
                 mid={0: lambda: post_mean(NT - 1), 1: lambda: post_var(NT - 1)})
        for b_ in (hcb, hib, lob):
            P.release(b_)
        P.release(ycb)
        P.release(ydb)

    def ple_prefetch(li):
        ptb = P.alloc("pTb", [128, 2, S], BF16)
        for k in range(2):
            direct_load(ptb, ptb.t[:, k, :], pT[li, k], ptb.k(k))
        wgb = P.alloc("wgate", [128, KC, D], BF16)
        direct_load(wgb, wgb.t[:].rearrange("p a b -> p (a b)"), wgate_d[li], wgb.k())
        wub = P.alloc("wup", [128, 2, D], BF16)
        direct_load(wub, wub.t[:].rearrange("p a b -> p (a b)"), wup_d[li], wub.k())
        return ptb, wgb, wub

    def ple(li, hnb, hook, prev, bufs):
        hn = hnb.t
        ptb, wgb, wub = bufs
        hook.start()
        for t in range(NT):
            for d in range(KC):
                bg, bgk = P.bank()
                bu, buk = P.bank()

                def mmg(e, bg=bg, d=d, t=t):
                    ins = None
                    for k in range(KC):
                        ins = e.matmul(bg[:], wgb.t[:, k, d * 128:(d + 1) * 128], hn[:, k, tsl(t)],
                                       start=(k == 0), stop=(k == KC - 1))
                    return ins
                if t == 0 and d == 3:
                    prev.flush()
                P.op("pe", mmg, r=[wgb.k()] + hn_keys(hnb, t), w=[bgk])

                def mmp(e, bu=bu, d=d, t=t):
                    ins = None
                    for k in range(2):
                        ins = e.matmul(bu[:], wub.t[:, k, d * 128:(d + 1) * 128], ptb.t[:, k, tsl(t)],
                                       start=(k == 0), stop=(k == 1))
                    return ins
                P.op("pe", mmp, r=[wub.k(), ptb.k(0), ptb.k(1)], w=[buk])
                sg = ft()
                P.op("act", lambda e, sg=sg, bg=bg: e.activation(out=sg.t[:], in_=bg[:], func=AF.Sigmoid),
                     w=[sg.k(), bgk])
                P.op("dve", lambda e, sg=sg, bu=bu: e.tensor_tensor(out=sg.t[:], in0=sg.t[:], in1=bu[:], op=ALU.mult),
                     r=[], w=[sg.k(), buk])
                P.op("dve", lambda e, sg=sg, d=d, t=t: e.tensor_tensor(out=H[:, d, tsl(t)], in0=H[:, d, tsl(t)],
                                                                      in1=sg.t[:], op=ALU.add),
                     r=[sg.k()], w=[Hb.k(d, t)])
            hook.tile_done(t)
        P.release(hnb)
        P.release(ptb)
        P.release(wgb)
        P.release(wub)

    phases = []
    for li in range(2):
        phases += [("ffn", li, 0), ("mix", li), ("ffn", li, 1), ("ple", li)]
    nph = len(phases) if stop is None else stop
    def gain_idx(ph):
        if ph[0] == "ffn":
            return (0 if ph[2] == 0 else 2) * 2 + ph[1]
        if ph[0] == "mix":
            return 2 + ph[1]
        return 6 + ph[1]

    ple_bufs = {}
    wv_pre = []
    hook0 = NormHook(gain_idx(phases[0]))
    hook0.start()
    issue_to(4)
    hook0.tile_done(0)
    hook0.tile_done(1)
    hook0.flush_one()
    hook0.pend = [2, 3]
    wsb = mix0_setup()
    cur_hn = hook0.hnb
    prev = hook0
    for pi in range(nph):
        ph = phases[pi]
        nxt = gain_idx(phases[pi + 1]) if pi + 1 < nph else (8 if stop is None else None)
        hook = NormHook(nxt, final=(pi + 1 == len(phases)), top=(pi in (3, 4)))
        if ph[0] == "ffn":
            pre = None
            if ph[2] == 1 and pi + 1 < nph:
                def pre(li=ph[1]):
                    ple_bufs[li] = ple_prefetch(li)
            if ph[1] == 0 and ph[2] == 0 and pi + 1 < nph:
                def pre():
                    wvb_ = P.alloc("wv", [128, KC, 512], BF16, top=True)
                    direct_load(wvb_, wvb_.t[:].rearrange("p a b -> p (a b)"), winV_d, wvb_.k())
                    wv_pre.append(wvb_)
            ffn(ph[1] * 2 + ph[2], cur_hn, hook, prev, pre)
        elif ph[0] == "mix":
            if ph[1] == 0:
                bfull = mix0_setup_b(wsb)
                mix0(cur_hn, hook, wsb, bfull, prev)
                P.release(wsb)
                P.release(bfull)
            else:
                mix1(cur_hn, hook, prev)
        else:
            ple(ph[1], cur_hn, hook, prev, ple_bufs[ph[1]])
        cur_hn = hook.hnb
        prev = hook
    prev.flush()

    finals = []
    if stop is None:
        pass
    return nc, P, dict(H=H, Hb=Hb, outT=outT, cur_hn=cur_hn, finals=finals, tsl=tsl, norm=None)


def _finish(nc, P, ctx, stop):
    H, Hb, outT = ctx["H"], ctx["Hb"], ctx["outT"]
    finals = []
    outT_v = outT.rearrange("k p s -> p k s")
    tsl = ctx["tsl"]
    for t in range(NT):
        for hf in range(2):
            ks = [k for k in range(KC) if k % 2 == hf]
            for k in ks:
                o = P.dma("sp", lambda e, t=t, k=k: e.dma_start(out=outT_v[:, k, tsl(t)], in_=H[:, k, tsl(t)]),
                          r=[Hb.k(k, t)])
                finals.append(o)
    P.finalize(finals)
    return nc


def _prep_shared(inp):
    f = np.float32
    g = {}
    gains = np.stack([inp["ffn1_norm"][0], inp["ffn1_norm"][1], inp["mix_norm"][0], inp["mix_norm"][1],
                      inp["ffn2_norm"][0], inp["ffn2_norm"][1], inp["ple_norm"][0], inp["ple_norm"][1],
                      inp["final_norm"]], 0)
    gains = gains.reshape(9, 8, 128).transpose(2, 0, 1).reshape(128, 72)

    def col4(v):
        return np.asarray(v).reshape(4, 128).T

    cst = np.zeros((128, CST_N), f)

    def put(name, arr):
        o, w = CST[name]
        cst[:, o:o + w] = np.asarray(arr, f).reshape(128, w)

    put("gains", gains)
    put("lng", col4(inp["gm_ln_g"][0]))
    put("lnb", col4(inp["gm_ln_b"][0]))
    put("scw", inp["sc_w"][0].reshape(3, 4, 128).transpose(2, 1, 0))
    put("pscale", col4(inp["pool_scale"][0]))
    put("cvw", inp["cv_w"][0].reshape(CVW, 4, 128).transpose(2, 1, 0))
    put("cvb", col4(inp["cv_b"][0]))
    put("cvg", col4(inp["cv_ln_g"][0]))
    put("cvbt", col4(inp["cv_ln_b"][0]))
    invc = np.zeros((4, 16), f)
    for gi in range(4):
        win = 2 << gi
        for j in range(16):
            invc[gi, j] = 1.0 / min(j + 1, win)
    put("invc", np.broadcast_to(invc.reshape(1, 64), (128, 64)))
    put("ident", np.eye(128, dtype=f))
    put("mask", np.triu(np.ones((128, 128), f)))
    put("bsb", np.broadcast_to(inp["gm_b_s"][0].reshape(1, 512), (128, 512)))
    g["cst"] = cst

    wgu = np.empty((4, FC, 128, 2048), f)
    wdn = np.empty((4, FC, 128, 1024), f)
    for li in range(2):
        for wi, (kgu, kdn) in enumerate((("ffn1_w_gu", "ffn1_w_down"), ("ffn2_w_gu", "ffn2_w_down"))):
            w = inp[kgu][li].reshape(8, 128, 2, FC, 128)
            wgu[li * 2 + wi] = w.transpose(3, 1, 0, 2, 4).reshape(FC, 128, 2048)
            wdn[li * 2 + wi] = inp[kdn][li].reshape(FC, 128, 1024)
    g["wgu"] = wgu
    g["wdn"] = wdn

    def kblock(w):
        n = w.shape[1]
        return w.reshape(8, 128, n).transpose(1, 0, 2)

    win0 = inp["ab_w_in"][0]
    wB = np.empty((4, 2, 128, 1536), f)
    for c in range(4):
        cols = np.concatenate([np.arange(1536 + c * 128, 1536 + (c + 1) * 128),
                               np.arange(2048 + c * 128, 2048 + (c + 1) * 128),
                               np.arange(1024 + c * 128, 1024 + (c + 1) * 128)])
        blk = kblock(win0[:, cols])
        wB[c, 0] = blk[:, 0:4].reshape(128, 1536)
        wB[c, 1] = blk[:, 4:8].reshape(128, 1536)
    g["winB"] = wB
    g["winV"] = np.ascontiguousarray(kblock(win0[:, 512:1024]).reshape(128, 4096))
    g["winU"] = np.stack([kblock(win0[:, ub * 256:(ub + 1) * 256]).reshape(128, 2048) for ub in range(2)])
    g["wout"] = np.stack([kblock(inp["ab_w_out"][0]).reshape(128, 8192), kblock(inp["cd_w_out"][0]).reshape(128, 8192)])
    g["wsT"] = np.ascontiguousarray(inp["gm_w_s"][0].transpose(2, 0, 1).reshape(128, 512))
    win1 = inp["cd_w_in"][0]
    g["winC"] = np.stack([kblock(win1[:, cb * 256:(cb + 1) * 256]).reshape(128, 2048) for cb in range(2)])
    wD = np.empty((4, 128, 2048), f)
    for c in range(4):
        cols = np.concatenate([np.arange(512 + c * 128, 512 + (c + 1) * 128),
                               np.arange(1024 + c * 128, 1024 + (c + 1) * 128)])
        wD[c] = kblock(win1[:, cols]).reshape(128, 2048)
    g["winD"] = wD
    g["poolw"] = np.ascontiguousarray(inp["pool_w"][0].transpose(1, 0, 2).reshape(128, 512))
    g["wgate"] = np.stack([kblock(inp["ple_w_gate"][li]).reshape(128, 8192) for li in range(2)])
    g["wup"] = np.stack([inp["ple_w_up"][li].reshape(2, 128, 1024).transpose(1, 0, 2).reshape(128, 2048)
                         for li in range(2)])
    return {k: np.ascontiguousarray(v, dtype=f) for k, v in g.items()}


def _prep_core(inp, b):
    xT = np.ascontiguousarray(np.asarray(inp["x"][b]).T.reshape(KC, 128, S), dtype=np.float32)
    pT = np.ascontiguousarray(np.asarray(inp["p"][:, b]).transpose(0, 2, 1).reshape(2, 2, 128, S), dtype=np.float32)
    return {"xT": xT, "pT": pT}


_CACHE = {}


def _get_program(stop=None):
    if stop not in _CACHE:
        nc, P, ctx = build_program(stop=stop)
        _finish(nc, P, ctx, stop)
        _CACHE[stop] = (nc, P)
    return _CACHE[stop]


def run(inputs, stop=None, trace=False, ncores=NB):
    inp = {k: np.asarray(v) for k, v in inputs.items()}
    shared = _prep_shared(inp)
    in_maps = []
    for b in range(ncores):
        m = dict(shared)
        m.update(_prep_core(inp, b))
        in_maps.append(m)
    nc, P = _get_program(stop)
    res = run_bass_kernel_spmd(nc, in_maps, core_ids=list(range(ncores)), trace=trace)
    out = np.stack([np.asarray(r["outT"]).reshape(D, S).T for r in res.results], 0)
    return np.ascontiguousarray(out, dtype=np.float32), res


def kernel(**inputs):
    out, _ = run(inputs)
    return out
```
